# Optimizing a Trainium2 kernel written in Bass

```python
import math
import jax, jax.numpy as jnp
from jax import lax
import numpy as np

D_MODEL = 2048
BATCH = 16
SEQ = 2048
DEPTH = 1
DEC_BATCH = 4
DEC_SEQ = 4096
PAST_LEN = 128

PLE_DIM = 256
NORM_EPS = 1e-6
HY_CH = D_MODEL // 2
SHORT_K = 3
FILT_BANDS = 16
FILT_EMB = 1 + 2 * FILT_BANDS
FILT_ORDER = 64
FILT_TARGET = 1e-2
FILT_FAST_PCT = 0.3
FILT_SLOW_PCT = 1.5
FILT_MAX_DECAY = math.log(FILT_TARGET) / FILT_FAST_PCT
FILT_MIN_DECAY = math.log(FILT_TARGET) / FILT_SLOW_PCT
ATT_WIDTH = D_MODEL // 2
N_HEADS = 8
HEAD_DIM = ATT_WIDTH // (2 * N_HEADS)
ROPE_DIM = HEAD_DIM // 4
ROPE_THETA = 500000.0
Q_BLOCK = 128
D_FF = ((8 * D_MODEL + 3 * 256 - 1) // (3 * 256)) * 256
HY_COLS = 3 * HY_CH
QK_COLS = ATT_WIDTH
V_COLS = ATT_WIDTH
GATE_COLS = 2 * D_MODEL
IN_COLS = HY_COLS + 2 * QK_COLS + V_COLS + GATE_COLS
SPLITS = (HY_COLS, HY_COLS + QK_COLS, HY_COLS + 2 * QK_COLS, HY_COLS + 2 * QK_COLS + V_COLS)

kernel_name = 'hyena_diffattn_gated_encoder'


def _rmsnorm(x, g):
    xf = x.astype(jnp.float32)
    y = xf * lax.rsqrt(jnp.mean(xf * xf, axis=-1, keepdims=True) + NORM_EPS)
    return (y * g.astype(jnp.float32)).astype(x.dtype)


def _rope(x):
    L = x.shape[1]
    inv = ROPE_THETA ** (-jnp.arange(0, ROPE_DIM, 2, dtype=jnp.float32) / ROPE_DIM)
    ang = jnp.arange(L, dtype=jnp.float32)[:, None] * inv[None, :]
    ang = jnp.concatenate([ang, ang], axis=-1)[None, :, None, None, :]
    xr = x[..., :ROPE_DIM].astype(jnp.float32)
    x1, x2 = jnp.split(xr, 2, axis=-1)
    rot = jnp.concatenate([-x2, x1], axis=-1)
    xr = xr * jnp.cos(ang) + rot * jnp.sin(ang)
    return jnp.concatenate([xr.astype(x.dtype), x[..., ROPE_DIM:]], axis=-1)


def _hyena_filters(L, w1, b1, w2, b2, freq, w3):
    f32 = jnp.float32
    t = jnp.linspace(0.0, 1.0, L, dtype=f32)[:, None]
    wpos = 2.0 * math.pi * jnp.arange(L, dtype=f32) / L
    bands = jnp.linspace(1e-4, FILT_BANDS - 1, FILT_BANDS, dtype=f32)
    ang = wpos[:, None] * bands[None, :]
    emb = jnp.concatenate([t, jnp.cos(ang), -jnp.sin(ang)], axis=-1)
    fr = freq.astype(f32)
    hdn = jnp.sin(fr * (emb @ w1.astype(f32) + b1.astype(f32)))
    hdn = jnp.sin(fr * (hdn @ w2.astype(f32) + b2.astype(f32)))
    h = (hdn @ w3.astype(f32)).reshape(L, 2, HY_CH)
    deltas = jnp.abs(jnp.linspace(FILT_MIN_DECAY, FILT_MAX_DECAY, HY_CH, dtype=f32))
    h = h * jnp.exp(-t * deltas[None, :])[:, None, :]
    h = h / (jnp.sum(jnp.abs(h), axis=(0, 1), keepdims=True) + NORM_EPS)
    return h[:, 0], h[:, 1]


def _long_conv(u, h_fwd, h_bwd, d):
    B, L, C = u.shape
    k = jnp.concatenate([h_fwd, jnp.zeros((1, C), jnp.float32), h_bwd[1:][::-1]], axis=0)
    uf32 = u.astype(jnp.float32)
    uf = jnp.fft.rfft(uf32, n=2 * L, axis=1)
    kf = jnp.fft.rfft(k, axis=0)
    y = jnp.fft.irfft(uf * kf[None], n=2 * L, axis=1)[:, :L]
    return (y + uf32 * d.astype(jnp.float32)).astype(u.dtype)


def _hyena_branch(u, conv_w, conv_b, w1, b1, w2, b2, freq, w3, hyena_d):
    L = u.shape[1]
    pad = SHORT_K // 2
    up = jnp.pad(u, ((0, 0), (pad, pad), (0, 0)))
    uc = conv_b
    for j in range(SHORT_K):
        uc = uc + up[:, j:j + L] * conv_w[j]
    x0, x1, v = jnp.split(uc, 3, axis=-1)
    h_fwd, h_bwd = _hyena_filters(L, w1, b1, w2, b2, freq, w3)
    return x0 * _long_conv(x1 * v, h_fwd, h_bwd, hyena_d)


def _diff_attention(q, k, v, lam_q1, lam_k1, lam_q2, lam_k2, g_subln, lam_init):
    B, L, _ = q.shape
    f32 = jnp.float32
    q = _rope(q.reshape(B, L, N_HEADS, 2, HEAD_DIM))
    k = _rope(k.reshape(B, L, N_HEADS, 2, HEAD_DIM))
    v = v.reshape(B, L, N_HEADS, 2 * HEAD_DIM)
    lam = (jnp.exp(jnp.sum(lam_q1.astype(f32) * lam_k1.astype(f32)))
           - jnp.exp(jnp.sum(lam_q2.astype(f32) * lam_k2.astype(f32))) + lam_init)
    scale = HEAD_DIM ** -0.5
    nb = L // Q_BLOCK
    qb = jnp.moveaxis(q.reshape(B, nb, Q_BLOCK, N_HEADS, 2, HEAD_DIM), 1, 0)

    def block(qi):
        s = jnp.einsum('bqhcd,bkhcd->bchqk', qi, k).astype(f32) * scale
        a = jax.nn.softmax(s, axis=-1)
        w = a[:, 0] - lam * a[:, 1]
        return jnp.einsum('bhqk,bkhe->bqhe', w.astype(v.dtype), v)

    o = lax.map(block, qb)
    o = jnp.moveaxis(o, 0, 1).reshape(B, L, N_HEADS, 2 * HEAD_DIM)
    o = _rmsnorm(o, g_subln) * (1.0 - lam_init)
    return o.reshape(B, L, ATT_WIDTH)


def _layer(x, p, lam_init, g_mix_pre, g_mix_post, g_ffn_pre, g_ffn_post, g_ple, w_in, b_gate,
           conv_w, conv_b, filt_w1, filt_b1, filt_w2, filt_b2, filt_freq, filt_w3, hyena_d,
           lam_q1, lam_k1, lam_q2, lam_k2, g_subln, w_hy_out, w_att_out, w_out,
           w_ffn_in, w_ffn_out, w_ple_in, w_ple_gate):
    B, L, _ = x.shape
    h = _rmsnorm(x, g_mix_pre)
    z = h @ w_in
    u_hy, q, k, v, g_logit = jnp.split(z, SPLITS, axis=-1)
    gates = jax.nn.sigmoid(g_logit.reshape(B, L, 2, D_MODEL) + b_gate)
    y_hy = _hyena_branch(u_hy, conv_w, conv_b, filt_w1, filt_b1, filt_w2, filt_b2,
                         filt_freq, filt_w3, hyena_d)
    y_att = _diff_attention(q, k, v, lam_q1, lam_k1, lam_q2, lam_k2, g_subln, lam_init)
    m = gates[:, :, 0] * (y_hy @ w_hy_out) + gates[:, :, 1] * (y_att @ w_att_out)
    x = x + _rmsnorm(m @ w_out, g_mix_post)
    h = _rmsnorm(x, g_ffn_pre)
    f_gate, f_up = jnp.split(h @ w_ffn_in, 2, axis=-1)
    x = x + _rmsnorm((jax.nn.silu(f_gate) * f_up) @ w_ffn_out, g_ffn_post)
    e_gate = jax.nn.sigmoid(_rmsnorm(x, g_ple) @ w_ple_gate)
    return x + (p @ w_ple_in) * e_gate


def _trunk(x, p, weights):
    for i in range(DEPTH):
        lam_init = 0.8 - 0.6 * math.exp(-0.3 * i)
        x = _layer(x, p[i], lam_init, *[w[i] for w in weights])
    return x


def setup_inputs(seed: int = 0) -> dict:
    key = jax.random.key(seed)
    ks = jax.random.split(key, 32)

    def nrm(k, shape, scale):
        return jax.random.normal(k, shape, jnp.float32) * scale

    def gain(k, n):
        return 1.0 + nrm(k, (DEPTH, n), 0.02)

    return {
        'x_prompt': nrm(ks[0], (BATCH, SEQ, D_MODEL), 1.0),
        'x_sample': nrm(ks[1], (DEC_BATCH, DEC_SEQ, D_MODEL), 1.0),
        'p_prompt': nrm(ks[2], (DEPTH, BATCH, SEQ, PLE_DIM), 1.0),
        'p_sample': nrm(ks[3], (DEPTH, DEC_BATCH, DEC_SEQ, PLE_DIM), 1.0),
        'g_mix_pre': gain(ks[4], D_MODEL),
        'g_mix_post': gain(ks[5], D_MODEL),
        'g_ffn_pre': gain(ks[6], D_MODEL),
        'g_ffn_post': gain(ks[7], D_MODEL),
        'g_ple': gain(ks[8], D_MODEL),
        'w_in': nrm(ks[9], (DEPTH, D_MODEL, IN_COLS), D_MODEL ** -0.5),
        'b_gate': nrm(ks[10], (DEPTH, 2, D_MODEL), 0.1),
        'conv_w': nrm(ks[11], (DEPTH, SHORT_K, HY_COLS), SHORT_K ** -0.5),
        'conv_b': nrm(ks[12], (DEPTH, HY_COLS), 0.02),
        'filt_w1': nrm(ks[13], (DEPTH, FILT_EMB, FILT_ORDER), FILT_EMB ** -0.5),
        'filt_b1': nrm(ks[14], (DEPTH, FILT_ORDER), 0.1),
        'filt_w2': nrm(ks[15], (DEPTH, FILT_ORDER, FILT_ORDER), FILT_ORDER ** -0.5),
        'filt_b2': nrm(ks[16], (DEPTH, FILT_ORDER), 0.1),
        'filt_freq': 1.0 + nrm(ks[17], (DEPTH, FILT_ORDER), 0.1),
        'filt_w3': nrm(ks[18], (DEPTH, FILT_ORDER, 2 * HY_CH), FILT_ORDER ** -0.5),
        'hyena_d': nrm(ks[19], (DEPTH, HY_CH), 0.5),
        'lam_q1': nrm(ks[20], (DEPTH, HEAD_DIM), 0.1),
        'lam_k1': nrm(ks[21], (DEPTH, HEAD_DIM), 0.1),
        'lam_q2': nrm(ks[22], (DEPTH, HEAD_DIM), 0.1),
        'lam_k2': nrm(ks[23], (DEPTH, HEAD_DIM), 0.1),
        'g_subln': gain(ks[24], 2 * HEAD_DIM),
        'w_hy_out': nrm(ks[25], (DEPTH, HY_CH, D_MODEL), HY_CH ** -0.5),
        'w_att_out': nrm(ks[26], (DEPTH, ATT_WIDTH, D_MODEL), ATT_WIDTH ** -0.5),
        'w_out': nrm(ks[27], (DEPTH, D_MODEL, D_MODEL), D_MODEL ** -0.5),
        'w_ffn_in': nrm(ks[28], (DEPTH, D_MODEL, 2 * D_FF), D_MODEL ** -0.5),
        'w_ffn_out': nrm(ks[29], (DEPTH, D_FF, D_MODEL), D_FF ** -0.5),
        'w_ple_in': nrm(ks[30], (DEPTH, PLE_DIM, D_MODEL), PLE_DIM ** -0.5),
        'w_ple_gate': nrm(ks[31], (DEPTH, D_MODEL, D_MODEL), D_MODEL ** -0.5),
    }


def reference(x_prompt, x_sample, p_prompt, p_sample, g_mix_pre, g_mix_post, g_ffn_pre,
              g_ffn_post, g_ple, w_in, b_gate, conv_w, conv_b, filt_w1, filt_b1, filt_w2,
              filt_b2, filt_freq, filt_w3, hyena_d, lam_q1, lam_k1, lam_q2, lam_k2, g_subln,
              w_hy_out, w_att_out, w_out, w_ffn_in, w_ffn_out, w_ple_in, w_ple_gate):
    weights = (g_mix_pre, g_mix_post, g_ffn_pre, g_ffn_post, g_ple, w_in, b_gate,
               conv_w, conv_b, filt_w1, filt_b1, filt_w2, filt_b2, filt_freq, filt_w3, hyena_d,
               lam_q1, lam_k1, lam_q2, lam_k2, g_subln, w_hy_out, w_att_out, w_out,
               w_ffn_in, w_ffn_out, w_ple_in, w_ple_gate)
    y_prompt = _trunk(x_prompt, p_prompt, weights)
    y_sample = _trunk(x_sample, p_sample, weights)
    return (y_prompt, y_sample)
```

```python
import math
import numpy as np
import ml_dtypes
import concourse.bass as bass
import concourse.mybir as mybir
from concourse.bass_utils import run_bass_kernel_spmd

F32 = mybir.dt.float32
BF16 = mybir.dt.bfloat16
AF = mybir.ActivationFunctionType
ALU = mybir.AluOpType
AX = mybir.AxisListType

EPOCH = 30000
NORM_EPS = 1e-6
TWO_PI = 2.0 * math.pi


class Buf:
    __slots__ = ("w", "r", "name")

    def __init__(self, name="", r=None):
        self.w = {}
        self.r = dict(r) if r else {}
        self.name = name


class Lane:
    def __init__(self, fw, name, step):
        self.fw = fw
        self.name = name
        self.step = step
        self.count = 0
        self.sems = []

    def sem_for(self, seq):
        ep = EPOCH // self.step
        e = (seq - 1) // ep
        while len(self.sems) <= e:
            self.sems.append(self.fw.nc.alloc_semaphore(f"s_{self.name}_{len(self.sems)}"))
        return self.sems[e], ((seq - 1) % ep + 1) * self.step


class Eng:
    def __init__(self, fw, name, is_compute):
        self.name = name
        self.instrs = []
        self.waited = {}
        self.lane = Lane(fw, name, 1) if is_compute else None
        self.dlanes = []
        self.dnext = 0


class FW:
    def __init__(self, nc, n_dma_lanes=8):
        self.nc = nc
        self.E = {
            "pe": Eng(self, "pe", True),
            "act": Eng(self, "act", True),
            "dve": Eng(self, "dve", True),
            "pool": Eng(self, "pool", True),
            "sp": Eng(self, "sp", False),
        }
        for q in ("sp", "pool"):
            self.E[q].dlanes = [Lane(self, f"d{q}{i}", 16) for i in range(n_dma_lanes)]
        self.snap = {}
        self.n_instr = 0
        self.marks = []
        self.pe_ops = 0
        self.tiny = False

    def all_lanes(self):
        for e in self.E.values():
            if e.lane is not None:
                yield e.lane
            for l in e.dlanes:
                yield l

    def mark(self, name):
        self.marks.append((name, self.pe_ops))

    def snapshot(self):
        self.snap = {l: l.count for l in self.all_lanes() if l.count > 0}

    def newbuf(self, name=""):
        return Buf(name, self.snap)

    def _need(self, eng, lane, seq, waits):
        if lane is eng.lane and lane.name == "pe":
            return
        if eng.waited.get(lane, 0) >= seq:
            return
        if waits.get(lane, 0) < seq:
            waits[lane] = seq

    def _deps(self, eng, reads, writes, partial):
        waits = {}
        for b in reads:
            for ln, sq in b.w.items():
                self._need(eng, ln, sq, waits)
        for b in writes:
            if not partial:
                for ln, sq in b.w.items():
                    self._need(eng, ln, sq, waits)
            for ln, sq in b.r.items():
                self._need(eng, ln, sq, waits)
        for ln, sq in waits.items():
            eng.waited[ln] = sq
            sem, val = ln.sem_for(sq)
            eng.instrs.append(("wait", sem, val))

    def _mark(self, lane, seq, reads, writes, partial):
        for b in reads:
            if b.r.get(lane, 0) < seq:
                b.r[lane] = seq
        for b in writes:
            if partial:
                b.w[lane] = seq
            else:
                b.w = {lane: seq}
                b.r = {}

    def op(self, ename, fn, reads=(), writes=(), inc=True, partial=False, pwrites=()):
        eng = self.E[ename]
        if ename == 'pe' and not self.tiny:
            self.pe_ops += 1
        if pwrites:
            self._deps(eng, (), pwrites, True)
        self._deps(eng, reads, writes, partial)
        lane = eng.lane
        seq = lane.count + 1
        if inc:
            lane.count = seq
            sem, _ = lane.sem_for(seq)
            eng.instrs.append(("op", fn, sem, 1))
        else:
            eng.instrs.append(("op", fn, None, 0))
        self._mark(lane, seq, reads, writes, partial)
        if pwrites:
            self._mark(lane, seq, (), pwrites, True)
        self.n_instr += 1

    def dma(self, qname, out, in_, reads=(), writes=(), partial=False, slow=False):
        eng = self.E[qname]
        lane = eng.dlanes[eng.dnext]
        eng.dnext = (eng.dnext + 1) % len(eng.dlanes)
        if lane.count > 0 and eng.waited.get(lane, 0) < lane.count:
            eng.waited[lane] = lane.count
            sem, val = lane.sem_for(lane.count)
            eng.instrs.append(("wait", sem, val))
        self._deps(eng, reads, writes, partial)
        seq = lane.count + 1
        lane.count = seq
        sem, _ = lane.sem_for(seq)
        if slow:
            eng.instrs.append(("op", lambda e, o=out, i=in_: e.dma_start(out=o, in_=i, allow_slow_non_contiguous=True), sem, 16))
        else:
            eng.instrs.append(("op", lambda e, o=out, i=in_: e.dma_start(out=o, in_=i), sem, 16))
        self._mark(lane, seq, reads, writes, partial)
        self.n_instr += 1

    def finish(self):
        self.stats = {n: (len(e.instrs), e.lane.count if e.lane else 0, [l.count for l in e.dlanes]) for n, e in self.E.items()}
        sp = self.E["sp"]
        for l in self.all_lanes():
            if l.count > 0 and sp.waited.get(l, 0) < l.count:
                sem, val = l.sem_for(l.count)
                sp.instrs.append(("wait", sem, val))
        nc = self.nc
        with nc.Block() as block:
            def replay(ename):
                def run(e):
                    for it in self.E[ename].instrs:
                        if it[0] == "wait":
                            e.wait_ge(it[1], it[2])
                        else:
                            ins = it[1](e)
                            if it[2] is not None:
                                ins.then_inc(it[2], it[3])
                return run
            block.tensor(replay("pe"))
            block.scalar(replay("act"))
            block.vector(replay("dve"))
            block.gpsimd(replay("pool"))
            block.sync(replay("sp"))


class Rot:
    def __init__(self, items):
        self.items = items
        self.i = 0

    def next(self):
        it = self.items[self.i]
        self.i = (self.i + 1) % len(self.items)
        return it


class Cfg:
    def __init__(self, D=2048, HY=1024, ATT=1024, DFF=5632, PLE=256, Lq=2048, ncores=8,
                 FE=33, FO=64, lam_init=0.2):
        self.D, self.HY, self.ATT, self.DFF, self.PLE, self.Lq = D, HY, ATT, DFF, PLE, Lq
        self.ncores = ncores
        self.FE, self.FO = FE, FO
        self.lam_init = lam_init
        self.KC, self.HC, self.AC, self.FC, self.PC = D // 128, HY // 128, ATT // 128, DFF // 128, PLE // 128
        self.NH = ATT // 128
        self.TT = min(512, Lq)
        self.NTG = Lq // self.TT
        self.NTB = Lq // 128
        self.CG = min(512, HY)
        self.NCG = HY // self.CG
        self.NCH = self.CG // 128
        self.QG = min(512, ATT)
        self.NQG = ATT // self.QG
        self.DG = min(512, D)
        self.NDG = D // self.DG
        self.LP = Lq
        self.LS = 2 * Lq
        self.IN_COLS = 3 * HY + 3 * ATT + 2 * D
        self.OX0, self.OX1, self.OV = 0, HY, 2 * HY
        self.OQ, self.OK, self.OVA = 3 * HY, 3 * HY + ATT, 3 * HY + 2 * ATT
        self.OG0 = 3 * HY + 3 * ATT
        self.OG1 = self.OG0 + D


FULL = Cfg()


def _bf(a):
    return np.ascontiguousarray(a.astype(ml_dtypes.bfloat16))


def dft_tables(order, tout):
    Lc = len(order)
    N = 2 * Lc
    half = N // 2
    t = order.astype(np.int64)[:, None]
    j = np.arange(half, dtype=np.int64)[None, :]
    ang = TWO_PI * ((t * j) % N).astype(np.float64) / N
    F = np.empty((Lc, N), np.float64)
    F[:, :half] = np.cos(ang)
    F[:, half:] = -np.sin(ang)
    F[:, half] = np.where(order % 2 == 0, 1.0, -1.0)
    to = tout.astype(np.int64)[None, :]
    jj = np.arange(half, dtype=np.int64)[:, None]
    ang2 = TWO_PI * ((jj * to) % N).astype(np.float64) / N
    G = np.empty((N, len(tout)), np.float64)
    G[:half] = 2.0 * np.cos(ang2) / N
    G[0] = 1.0 / N
    G[half:] = -2.0 * np.sin(ang2) / N
    G[half] = np.where(tout % 2 == 0, 1.0, -1.0) / N
    nt, nf, ntb = Lc // 128, N // 128, len(tout) // 128
    Ft = F.reshape(nt, 128, nf, 128).transpose(2, 1, 0, 3)
    Gt = G.reshape(nf, 128, ntb, 128).transpose(2, 1, 0, 3)
    return _bf(Ft), _bf(Gt)


def filter_tables(L, order, HY, FB=16):
    f32 = np.float32
    t_all = np.linspace(0.0, 1.0, L, dtype=f32)
    t = t_all[order]
    wpos = (f32(TWO_PI) * order.astype(f32) / f32(L)).astype(f32)
    bands = np.linspace(1e-4, FB - 1, FB, dtype=f32)
    ang = wpos[:, None] * bands[None, :]
    emb = np.concatenate([t[:, None], np.cos(ang), -np.sin(ang)], axis=-1).astype(f32)
    max_decay = math.log(1e-2) / 0.3
    min_decay = math.log(1e-2) / 1.5
    deltas = np.abs(np.linspace(min_decay, max_decay, HY, dtype=f32))
    dec = np.exp(-t[:, None] * deltas[None, :]).astype(f32)
    e0 = (order == 0).astype(f32)[None, :]
    return np.ascontiguousarray(emb.T), np.ascontiguousarray(dec.T), np.ascontiguousarray(e0)


def rope_table(pos, RD=16, theta=500000.0):
    inv = theta ** (-np.arange(0, RD, 2, dtype=np.float32) / RD)
    ang = pos.astype(np.float32)[:, None] * inv[None, :]
    return np.concatenate([np.cos(ang), np.sin(ang)], axis=-1).astype(np.float32)


def build(cfg, debug=False):
    c = cfg
    D, HY, ATT, DFF, PLE, Lq = c.D, c.HY, c.ATT, c.DFF, c.PLE, c.Lq
    KC, HC, AC, FC, PC, NH = c.KC, c.HC, c.AC, c.FC, c.PC, c.NH
    TT, NTG, NTB, CG, NCG, NCH = c.TT, c.NTG, c.NTB, c.CG, c.NCG, c.NCH
    QG, NQG, DG, NDG = c.QG, c.NQG, c.DG, c.NDG
    LP, LS = c.LP, c.LS
    nc = bass.Bass("TRN2", target_bir_lowering=False)
    fw = FW(nc)
    skind = "ExternalOutput" if debug else "Internal"
    STQ = "pool"

    def din(name, shape, dt=F32):
        return nc.dram_tensor(name, list(shape), dt, kind="ExternalInput").ap()

    def dscr(name, shape, dt):
        return nc.dram_tensor(name, list(shape), dt, kind=skind).ap()

    xq = din("xq", [3, Lq, D])
    xo = din("xo", [Lq, D])
    xhalo = din("xhalo", [4, 2, D])
    pq = din("pq", [3, Lq, PLE])
    rope = din("rope", [4, Lq, 16])
    nfP, ntP, nfS, ntS = 2 * LP // 128, LP // 128, 2 * LS // 128, LS // 128
    FtP = din("FtP", [nfP, 128, ntP, 128], BF16)
    GtP = din("GtP", [NTB, 128, nfP, 128], BF16)
    FtS = din("FtS", [nfS, 128, ntS, 128], BF16)
    GtS = din("GtS", [NTB, 128, nfS, 128], BF16)
    embP = din("embP", [c.FE, LP])
    embS = din("embS", [c.FE, LS])
    decP = din("decP", [HY, LP])
    decS = din("decS", [HY, LS])
    e0P = din("e0P", [1, LP])
    e0S = din("e0S", [1, LS])
    W = {}
    for name, shape in [
        ("g_mix_pre", [1, D]), ("g_mix_post", [1, D]), ("g_ffn_pre", [1, D]), ("g_ffn_post", [1, D]),
        ("g_ple", [1, D]), ("w_in", [D, c.IN_COLS]), ("b_gate", [2, D]), ("conv_w", [3, 3 * HY]),
        ("conv_b", [1, 3 * HY]), ("filt_w1", [c.FE, c.FO]), ("filt_b1", [1, c.FO]), ("filt_w2", [c.FO, c.FO]),
        ("filt_b2", [1, c.FO]), ("filt_freq", [1, c.FO]), ("filt_w3", [c.FO, 2 * HY]), ("hyena_d", [1, HY]),
        ("lam_q1", [1, 64]), ("lam_k1", [1, 64]), ("lam_q2", [1, 64]), ("lam_k2", [1, 64]),
        ("g_subln", [1, 128]), ("w_hy_out", [HY, D]), ("w_att_out", [ATT, D]), ("w_out", [D, D]),
        ("w_ffn_in", [D, 2 * DFF]), ("w_ffn_out", [DFF, D]), ("w_ple_in", [PLE, D]), ("w_ple_gate", [D, D]),
    ]:
        W[name] = din(name, shape)
    yq = nc.dram_tensor("yq", [3, Lq, D], F32, kind="ExternalOutput").ap()

    kfP = dscr("kfP", [2, LP, HY], F32)
    kfS = dscr("kfS", [2, LS, HY], F32)
    x0T_s = dscr("x0T_s", [HY, Lq], BF16)
    u_s = dscr("u_s", [NCG, LS, CG], BF16)
    qT_s = dscr("qT_s", [ATT, Lq], BF16)
    kT_s = dscr("kT_s", [ATT, LS], BF16)
    v_s = dscr("v_s", [LS, NH, 129], BF16)
    gates_s = dscr("gates_s", [2, D, Lq], F32)
    yhyT_s = dscr("yhyT_s", [HY, Lq], BF16)
    yattT_s = dscr("yattT_s", [ATT, Lq], BF16)
    mT_s = dscr("mT_s", [D, Lq], BF16)
    br_s = dscr("br_s", [Lq, D], F32)
    x1_s = dscr("x1_s", [Lq, D], F32)
    x2_s = dscr("x2_s", [Lq, D], F32)
    aT_s = dscr("aT_s", [DFF, Lq], BF16)
    wfo16 = nc.dram_tensor("wfo16", [DFF, D], BF16).ap()

    def dbuf(name):
        return Buf(name)
    B_kfP, B_kfS = dbuf("kfP"), dbuf("kfS")
    B_x0T, B_u, B_qT, B_kT, B_v = dbuf("x0T"), dbuf("u"), dbuf("qT"), dbuf("kT"), dbuf("v")
    B_gates, B_yhyT, B_yattT, B_mT = dbuf("gates"), dbuf("yhyT"), dbuf("yattT"), dbuf("mT")
    B_br, B_x1, B_x2, B_aT, B_out = dbuf("br"), dbuf("x1"), dbuf("x2"), dbuf("aT"), dbuf("out")
    B_wfo = dbuf("wfo16")

    def sb(name, shape, dt=F32):
        return nc.alloc_sbuf_tensor(name, list(shape), dt)
    ident = sb("ident", [128, 128], BF16)
    identf = sb("identf", [128, 128])
    gcol = sb("gcol", [128, 3, KC])
    cwT = sb("cwT", [128, 3, 3 * HC])
    cbT = sb("cbT", [128, 3 * HC])
    bgT = sb("bgT", [128, 2, KC])
    dcol = sb("dcol", [128, HC])
    cst = sb("cst", [128, 8])
    gsub = sb("gsub", [128, 128])
    lamt = sb("lamt", [128, 4, 64])
    lams = sb("lams", [128, 4])
    B_const = Buf("const")
    B_ident = Buf("ident")

    banks = [nc.alloc_psum_tensor(f"pb{i}", [128, 512], F32) for i in range(8)]
    bankB = [Buf(f"pb{i}") for i in range(8)]
    psum_all = Rot(list(zip(banks, bankB)))

    AW = (nc.sbuf_bytes_remaining - 4096) // 4
    AW -= AW % 8
    arena_t = nc.alloc_sbuf_tensor("arena", [128, AW], F32)
    ar = {"off": 0}

    def areset(to=0):
        ar["off"] = to
        fw.snapshot()

    def aalloc(shape, dt=F32, name=""):
        n = int(np.prod(shape))
        words = (n * (2 if dt == BF16 else 4) + 3) // 4
        words += (-words) % 8
        off = ar["off"]
        assert off + words <= AW, f"arena overflow {name}: {off}+{words}>{AW}"
        ar["off"] = off + words
        ap = arena_t[:, off:off + words]
        if dt != F32:
            ap = ap.bitcast(dt)
        ap = ap[:, 0:n]
        if len(shape) == 2:
            ap = ap.rearrange("p (a b) -> p a b", a=shape[0])
        elif len(shape) == 3:
            ap = ap.rearrange("p (a b c) -> p a b c", a=shape[0], b=shape[1])
        return ap, fw.newbuf(name)

    def apool(n, shape, dt=F32, name=""):
        return Rot([aalloc(shape, dt, f"{name}{i}") for i in range(n)])

    def mm_group(out_ap, pairs, reads, writes):
        n = len(pairs)
        for i, (l, r) in enumerate(pairs):
            fw.op("pe", lambda e, l=l, r=r, i=i: e.matmul(out_ap, lhsT=l, rhs=r, start=(i == 0), stop=(i == n - 1)),
                  reads=reads, writes=writes, inc=(i == n - 1))

    def transpose_group(out_aps, in_aps, reads, writes):
        n = len(out_aps)
        for i in range(n):
            fw.op("pe", lambda e, o=out_aps[i], a=in_aps[i]: e.transpose(out=o, in_=a, identity=ident[:]),
                  reads=list(reads) + [B_ident], writes=writes, inc=(i == n - 1))

    def act(out, in_, func, reads, writes, partial=False, pwrites=(), **kw):
        fw.op("act", lambda e: e.activation(out=out, in_=in_, func=func, **kw), reads=reads, writes=writes, partial=partial, pwrites=pwrites)

    def tt(eng, out, in0, in1, op, reads, writes, partial=False):
        fw.op(eng, lambda e: e.tensor_tensor(out=out, in0=in0, in1=in1, op=op), reads=reads, writes=writes, partial=partial)

    def ts(eng, out, in0, s1, s2, op0, op1, reads, writes, partial=False):
        if op1 is None:
            fw.op(eng, lambda e: e.tensor_scalar(out=out, in0=in0, scalar1=s1, scalar2=None, op0=op0), reads=reads, writes=writes, partial=partial)
        else:
            fw.op(eng, lambda e: e.tensor_scalar(out=out, in0=in0, scalar1=s1, scalar2=s2, op0=op0, op1=op1), reads=reads, writes=writes, partial=partial)

    def stt(eng, out, in0, scalar, in1, op0, op1, reads, writes, partial=False):
        fw.op(eng, lambda e: e.scalar_tensor_tensor(out=out, in0=in0, scalar=scalar, in1=in1, op0=op0, op1=op1),
              reads=reads, writes=writes, partial=partial)

    def cp(eng, out, in_, reads, writes, partial=False):
        if eng == "act":
            act(out, in_, AF.Copy, reads, writes, partial)
        else:
            fw.op(eng, lambda e: e.tensor_copy(out=out, in_=in_), reads=reads, writes=writes, partial=partial)

    def wload(dst, dstB, w_ap, k0, nk, c0, ncols, part=False):
        src = w_ap[k0 * 128:(k0 + nk) * 128, c0:c0 + ncols].rearrange("(c p) n -> p c n", p=128)
        fw.dma("pool", dst, src, writes=[dstB], partial=part)

    def prefetched(keys, loader, depth=1):
        keys = list(keys)
        handles = {}
        nxt = 0
        for idx, k in enumerate(keys):
            while nxt < len(keys) and nxt <= idx + depth:
                handles[nxt] = loader(keys[nxt])
                nxt += 1
            yield k, handles.pop(idx)

    def rstd_from_ss(st, stB, col_ss, col_out, n):
        ts("dve", st[:, col_out:col_out + 1], st[:, col_ss:col_ss + 1], 1.0 / n, NORM_EPS, ALU.mult, ALU.add, [stB], [stB])
        act(st[:, col_out:col_out + 1], st[:, col_out:col_out + 1], AF.Sqrt, [stB], [stB])
        fw.op("dve", lambda e: e.reciprocal(out=st[:, col_out:col_out + 1], in_=st[:, col_out:col_out + 1]), reads=[stB], writes=[stB])

    fw.op("pool", lambda e: e.memset(identf[:], 0.0), writes=[B_ident])
    fw.op("pool", lambda e: e.affine_select(out=identf[:], in_=identf[:], pattern=[[-1, 128]], compare_op=ALU.not_equal,
                                            fill=1.0, base=0, channel_multiplier=1), reads=[B_ident], writes=[B_ident])
    fw.op("pool", lambda e: e.tensor_copy(out=ident[:], in_=identf[:]), reads=[B_ident], writes=[B_ident])
    for i, gname in enumerate(["g_mix_pre", "g_ffn_pre", "g_ple"]):
        fw.dma("sp", gcol[:, i, :], W[gname].rearrange("o (c p) -> p (o c)", p=128), writes=[B_const], partial=True, slow=True)
    fw.dma("sp", cwT[:], W["conv_w"].rearrange("j (c p) -> p j c", p=128), writes=[B_const], partial=True, slow=True)
    fw.dma("sp", cbT[:], W["conv_b"].rearrange("o (c p) -> p (o c)", p=128), writes=[B_const], partial=True, slow=True)
    fw.dma("sp", bgT[:], W["b_gate"].rearrange("j (c p) -> p j c", p=128), writes=[B_const], partial=True, slow=True)
    fw.dma("sp", dcol[:], W["hyena_d"].rearrange("o (c p) -> p (o c)", p=128), writes=[B_const], partial=True, slow=True)
    fw.dma("sp", gsub[:], W["g_subln"].partition_broadcast(128), writes=[B_const], partial=True)
    for i, nm in enumerate(["lam_q1", "lam_k1", "lam_q2", "lam_k2"]):
        fw.dma("sp", lamt[:, i, :], W[nm].partition_broadcast(128), writes=[B_const], partial=True)
    fw.op("pool", lambda e: e.memset(cst[:, 0:1], -math.pi), writes=[B_const], partial=True)
    fw.op("pool", lambda e: e.memset(cst[:, 3:4], NORM_EPS), writes=[B_const], partial=True)
    tt("dve", lamt[:, 0, :], lamt[:, 0, :], lamt[:, 1, :], ALU.mult, [B_const], [B_const])
    tt("dve", lamt[:, 2, :], lamt[:, 2, :], lamt[:, 3, :], ALU.mult, [B_const], [B_const])
    fw.op("dve", lambda e: e.reduce_sum(out=lams[:, 0:1], in_=lamt[:, 0, :], axis=AX.X), reads=[B_const], writes=[B_const])
    fw.op("dve", lambda e: e.reduce_sum(out=lams[:, 1:2], in_=lamt[:, 2, :], axis=AX.X), reads=[B_const], writes=[B_const])
    act(lams[:, 0:2], lams[:, 0:2], AF.Exp, [B_const], [B_const])
    tt("dve", lams[:, 2:3], lams[:, 0:1], lams[:, 1:2], ALU.subtract, [B_const], [B_const])
    ts("dve", cst[:, 1:2], lams[:, 2:3], c.lam_init, None, ALU.add, None, [B_const], [B_const])
    ts("dve", cst[:, 2:3], cst[:, 1:2], -1.0, None, ALU.mult, None, [B_const], [B_const])
    ts("dve", gsub[:], gsub[:], 1.0 - c.lam_init, None, ALU.mult, None, [B_const], [B_const])

    def filter_phase(L, emb_d, dec_d, e0_d, Ft_d, kf_d, B_kf):
        fw.mark("filter_phase")
        N = 2 * L
        nt, nf = L // 128, N // 128
        half = nf // 2
        FE, FO = c.FE, c.FO
        TL = min(512, L)
        NTL = L // TL
        OFF = 7.0 * math.pi
        areset()
        h2, Bh2 = aalloc([L], F32, "h2")
        e0t, Be0 = aalloc([L], BF16, "e0t")
        ome0, Bome0 = aalloc([L], BF16, "ome0")
        w3t, Bw3 = aalloc([2 * HY], F32, "w3t")
        mark = ar["off"]
        w1t, Bw1 = aalloc([FO], F32, "w1t")
        w2t, Bw2 = aalloc([FO], F32, "w2t")
        fcol, Bfc = aalloc([8], F32, "fcol")
        embt, Bemb = aalloc([L], F32, "embt")
        h1, Bh1 = aalloc([L], F32, "h1")
        tmpp = apool(2, [TL], F32, "ftmp")
        fw.dma("sp", w1t[0:FE, :], W["filt_w1"], writes=[Bw1])
        fw.dma("sp", w2t[0:FO, :], W["filt_w2"], writes=[Bw2])
        fw.dma("sp", w3t[0:FO, :], W["filt_w3"], writes=[Bw3])
        fw.dma("sp", embt[0:FE, :], emb_d, writes=[Bemb])
        fw.dma("sp", fcol[0:FO, 0:1], W["filt_freq"].rearrange("o f -> f o"), writes=[Bfc], partial=True, slow=True)
        fw.dma("sp", fcol[0:FO, 1:2], W["filt_b1"].rearrange("o f -> f o"), writes=[Bfc], partial=True, slow=True)
        fw.dma("sp", fcol[0:FO, 2:3], W["filt_b2"].rearrange("o f -> f o"), writes=[Bfc], partial=True, slow=True)
        fw.dma("pool", e0t[:], e0_d.partition_broadcast(128), writes=[Be0])
        ts("dve", ome0[:], e0t[:], -1.0, 1.0, ALU.mult, ALU.add, [Be0], [Bome0])
        for k in (1, 2):
            ts("dve", fcol[0:FO, 2 + k:3 + k], fcol[0:FO, k:k + 1], fcol[0:FO, 0:1], None, ALU.mult, None, [Bfc], [Bfc])

        I32 = mybir.dt.int32
        kip = apool(2, [TL], I32, "fki")
        kfp = apool(2, [TL], F32, "fkf")
        PI_SAFE = 3.14159

        def sin_layer(wt, Bw, K, src, Bsrc, dst, Bdst, kcol):
            for tl in range(NTL):
                pb, pB = psum_all.next()
                sl = slice(tl * TL, (tl + 1) * TL)
                mm_group(pb[0:FO, 0:TL], [(wt[0:K, 0:FO], src[0:K, sl])], [Bw, Bsrc], [pB])
                tm, tB = tmpp.next()
                ts("dve", tm[0:FO, :], pb[0:FO, 0:TL], fcol[0:FO, 0:1], fcol[0:FO, 2 + kcol:3 + kcol], ALU.mult, ALU.add, [pB, Bfc], [tB])
                ki, kiB = kip.next()
                ts("dve", ki[0:FO, :], tm[0:FO, :], 1.0 / TWO_PI, None, ALU.mult, None, [tB], [kiB])
                kf, kfB = kfp.next()
                cp("dve", kf[0:FO, :], ki[0:FO, :], [kiB], [kfB])
                stt("dve", tm[0:FO, :], kf[0:FO, :], -TWO_PI, tm[0:FO, :], ALU.mult, ALU.add, [kfB, tB], [tB])
                ts("dve", tm[0:FO, :], tm[0:FO, :], -PI_SAFE, PI_SAFE, ALU.max, ALU.min, [tB], [tB])
                act(dst[0:FO, sl], tm[0:FO, :], AF.Sin, [tB], [Bdst], partial=True)

        sin_layer(w1t, Bw1, FE, embt, Bemb, h1, Bh1, 1)
        sin_layer(w2t, Bw2, FO, h1, Bh1, h2, Bh2, 2)

        for g in range(NCG):
            areset(mark)
            hf_tm, Bhf = aalloc([nt, 2, CG], BF16, "hf_tm")
            hw_, Bhw = [None, None], [None, None]
            hw_[0], Bhw[0] = aalloc([L], F32, "hfw")
            hw_[1], Bhw[1] = aalloc([L], F32, "hbw")
            hb16 = apool(2, [L], BF16, "hb16")
            decp = apool(2, [TL], F32, "decp")
            junk, Bjunk = aalloc([TL], BF16, "fjunk")
            asum, Bas = aalloc([2 * NTL + 4], F32, "asum")
            fpool = apool(2, [nt, 128], BF16, "fblk")
            outp = apool(3, [CG], F32, "fout")
            for j in range(NCH):
                ch = g * NCH + j
                for d_ in range(2):
                    col0 = d_ * HY + ch * 128
                    for tl in range(NTL):
                        sl = slice(tl * TL, (tl + 1) * TL)
                        pb, pB = psum_all.next()
                        mm_group(pb[:, 0:TL], [(w3t[0:FO, col0:col0 + 128], h2[0:FO, sl])], [Bw3, Bh2], [pB])
                        dt_, dB = decp.next()
                        fw.dma("sp", dt_[:], dec_d[ch * 128:(ch + 1) * 128, sl], writes=[dB])
                        tt("dve", hw_[d_][:, sl], pb[:, 0:TL], dt_[:], ALU.mult, [pB, dB], [Bhw[d_]], partial=(tl > 0))
                        act(junk[:], hw_[d_][:, sl], AF.Abs, [Bhw[d_]], [Bjunk, Bas], accum_out=asum[:, d_ * NTL + tl:d_ * NTL + tl + 1])
                na = 2 * NTL
                fw.op("dve", lambda e, na=na: e.reduce_sum(out=asum[:, na:na + 1], in_=asum[:, 0:na], axis=AX.X), reads=[Bas], writes=[Bas])
                ts("dve", asum[:, na:na + 1], asum[:, na:na + 1], NORM_EPS, None, ALU.add, None, [Bas], [Bas])
                fw.op("dve", lambda e, na=na: e.reciprocal(out=asum[:, na + 1:na + 2], in_=asum[:, na:na + 1]), reads=[Bas], writes=[Bas])
                rinv = asum[:, na + 1:na + 2]
                ts("dve", hw_[0][:], hw_[0][:], rinv, None, ALU.mult, None, [Bhw[0], Bas], [Bhw[0]])
                act(hw_[1][:], hw_[1][:], AF.Copy, [Bhw[1], Bas], [Bhw[1]], scale=rinv)
                for col in ([0] if L == Lq else [0, Lq]):
                    stt("dve", hw_[0][:, col:col + 1], e0t[:, col:col + 1], dcol[:, ch:ch + 1], hw_[0][:, col:col + 1], ALU.mult, ALU.add,
                        [Be0, B_const, Bhw[0]], [Bhw[0]])
                    tt("dve", hw_[1][:, col:col + 1], hw_[1][:, col:col + 1], ome0[:, col:col + 1], ALU.mult, [Bhw[1], Bome0], [Bhw[1]])
                hsb, BhsB = hb16.next()
                tt("pool", hsb[:], hw_[0][:], hw_[1][:], ALU.add, [Bhw[0], Bhw[1]], [BhsB])
                hdb, BhdB = hb16.next()
                tt("dve", hdb[:], hw_[0][:], hw_[1][:], ALU.subtract, [Bhw[0], Bhw[1]], [BhdB])
                for d_, (src, sB) in enumerate([(hsb, BhsB), (hdb, BhdB)]):
                    for tc0 in range(0, nt, 4):
                        n4 = min(4, nt - tc0)
                        pb, pB = psum_all.next()
                        pv = pb[:].bitcast(BF16)
                        transpose_group([pv[:, i * 128:(i + 1) * 128] for i in range(n4)],
                                        [src[:, (tc0 + i) * 128:(tc0 + i + 1) * 128] for i in range(n4)], [sB], [pB])
                        cp("act" if (tc0 // 4) % 2 else "dve", hf_tm[:, tc0:tc0 + n4, d_, j * 128:(j + 1) * 128],
                           pv[:, 0:n4 * 128].rearrange("p (a b) -> p a b", a=n4), [pB], [Bhf], partial=True)
            for fc in range(nf):
                fb, fB = fpool.next()
                fw.dma("sp", fb[:], Ft_d[fc], writes=[fB])
                pf, pfB = psum_all.next()
                sel = 0 if fc < half else 1
                mm_group(pf[:, 0:CG], [(fb[:, tc, :], hf_tm[:, tc, sel, :]) for tc in range(nt)], [fB, Bhf], [pfB])
                ot, oB = outp.next()
                cp("act" if fc % 2 else "dve", ot[:], pf[:, 0:CG], [pfB], [oB])
                if fc == half:
                    pn, pnB = psum_all.next()
                    mm_group(pn[0:1, 0:CG], [(fb[:, tc, 0:1], hf_tm[:, tc, 0, :]) for tc in range(nt)], [fB, Bhf], [pnB])
                    cp("dve", ot[0:1, :], pn[0:1, 0:CG], [pnB, oB], [oB])
                fw.dma(STQ, kf_d[fc // half, (fc % half) * 128:(fc % half + 1) * 128, g * CG:(g + 1) * CG], ot[:],
                       reads=[oB], writes=[B_kf], partial=True)

    def norm_phase(x_d, add_d, B_add, gpost_name, store_d, B_store, gidx, hT, BhT, col0, halo_d=None, p_d=None, pT=None, BpT=None, B_x=None):
        fw.mark("norm_phase")
        has_add = add_d is not None
        xin = apool(5, [D], F32, "xin")
        hbp = apool(2, [D], BF16, "hb")
        stp = apool(8, [8], F32, "nst")
        junk, Bjunk = aalloc([D], BF16, "njunk")
        if has_add:
            bin_ = apool(3, [D], F32, "bin")
            gpt, Bgp = aalloc([D], F32, "gpost")
            fw.dma("sp", gpt[:], W[gpost_name].partition_broadcast(128), writes=[Bgp])
        if p_d is not None:
            pin = apool(2, [PLE], F32, "pin")
            pbf = apool(2, [PLE], BF16, "pbf")
        nblk = NTB + (1 if halo_d is not None else 0)
        ctx = {}

        def ok(j):
            return 0 <= j < nblk

        def load(tb):
            is_halo = tb == NTB
            xt, xB = xin.next()
            st, sB = stp.next()
            d = ctx[tb] = dict(xt=xt, xB=xB, st=st, sB=sB, halo=is_halo)
            if is_halo:
                fw.op("pool", lambda e, xt=xt: e.memset(xt[:], 0.0), writes=[xB])
                fw.dma("sp", xt[0:2, :], halo_d, reads=[xB], writes=[xB], partial=True)
            else:
                fw.dma("sp", xt[:], x_d[tb * 128:(tb + 1) * 128, :], reads=([B_x] if B_x is not None else []), writes=[xB])
            if has_add:
                bt, bB = bin_.next()
                d.update(bt=bt, bB=bB)
                fw.dma("sp", bt[:], add_d[tb * 128:(tb + 1) * 128, :], reads=[B_add], writes=[bB])

        def sq(tb, which):
            d = ctx[tb]
            st, sB = d["st"], d["sB"]
            if which == "b":
                act(junk[:], d["bt"][:], AF.Square, [d["bB"]], [Bjunk], pwrites=[sB], accum_out=st[:, 0:1])
            else:
                act(junk[:], d["xt"][:], AF.Square, [d["xB"]], [Bjunk], pwrites=[sB], accum_out=st[:, 2:3])

        def r1(tb, c0_):
            d = ctx[tb]
            st, sB = d["st"], d["sB"]
            ts("dve", st[:, c0_ + 1:c0_ + 2], st[:, c0_:c0_ + 1], 1.0 / D, NORM_EPS, ALU.mult, ALU.add, [sB], [sB], partial=True)
            act(st[:, c0_ + 1:c0_ + 2], st[:, c0_ + 1:c0_ + 2], AF.Sqrt, [sB], [sB], partial=True)

        def r2(tb, c0_):
            d = ctx[tb]
            st, sB = d["st"], d["sB"]
            fw.op("dve", lambda e, st=st, c0_=c0_: e.reciprocal(out=st[:, c0_ + 1:c0_ + 2], in_=st[:, c0_ + 1:c0_ + 2]), reads=[sB], writes=[sB], partial=True)

        def combine(tb):
            d = ctx[tb]
            xt, xB, st, sB, bt, bB = d["xt"], d["xB"], d["st"], d["sB"], d["bt"], d["bB"]
            stt("dve", bt[:], bt[:], st[:, 1:2], gpt[:], ALU.mult, ALU.mult, [bB, sB, Bgp], [bB])
            tt("pool", xt[:], xt[:], bt[:], ALU.add, [xB, bB], [xB])
            if store_d is not None:
                fw.dma(STQ, store_d[tb * 128:(tb + 1) * 128, :], xt[:], reads=[xB], writes=[B_store], partial=True)

        def scale(tb):
            d = ctx[tb]
            hb, hB = hbp.next()
            d.update(hb=hb, hB=hB, pts=[])
            ts("dve", hb[:], d["xt"][:], d["st"][:, 3:4], None, ALU.mult, None, [d["xB"], d["sB"]], [hB])
            for c0 in range(0, KC, 4):
                n4 = min(4, KC - c0)
                pb, pB = psum_all.next()
                pv = pb[:].bitcast(BF16)
                transpose_group([pv[:, i * 128:(i + 1) * 128] for i in range(n4)],
                                [hb[:, (c0 + i) * 128:(c0 + i + 1) * 128] for i in range(n4)], [hB], [pB])
                d["pts"].append((c0, n4, pv, pB))

        def evac(tb):
            d = ctx.pop(tb)
            for c0, n4, pv, pB in d["pts"]:
                pv3 = pv[:, 0:n4 * 128].rearrange("p (a b) -> p a b", a=n4)
                gb = gcol[:, gidx, c0:c0 + n4]
                if d["halo"]:
                    for hc, dc_ in ((0, 0), (1, Lq + 1)):
                        tt("dve", hT[:, c0:c0 + n4, dc_:dc_ + 1], pv3[:, :, hc:hc + 1], gb.unsqueeze(2), ALU.mult,
                           [pB, B_const], [BhT], partial=True)
                else:
                    tt("dve", hT[:, c0:c0 + n4, col0 + tb * 128:col0 + (tb + 1) * 128], pv3,
                       gb.unsqueeze(2).to_broadcast([128, n4, 128]), ALU.mult, [pB, B_const], [BhT], partial=True)
            if p_d is not None and not d["halo"]:
                pt_, ptB = pin.next()
                fw.dma("sp", pt_[:], p_d[tb * 128:(tb + 1) * 128, :], writes=[ptB])
                pb16, pbB = pbf.next()
                cp("pool", pb16[:], pt_[:], [ptB], [pbB])
                pb, pB = psum_all.next()
                pv = pb[:].bitcast(BF16)
                transpose_group([pv[:, i * 128:(i + 1) * 128] for i in range(PC)],
                                [pb16[:, i * 128:(i + 1) * 128] for i in range(PC)], [pbB], [pB])
                cp("act", pT[:, :, tb * 128:(tb + 1) * 128], pv[:, 0:PC * 128].rearrange("p (a b) -> p a b", a=PC), [pB], [BpT], partial=True)

        for k in range(nblk + 4):
            if ok(k):
                load(k)
                sq(k, "b" if has_add else "x")
            if has_add:
                if ok(k - 2):
                    r1(k - 2, 2)
                if ok(k):
                    r1(k, 0)
                if ok(k - 2):
                    r2(k - 2, 2)
                if ok(k):
                    r2(k, 0)
                if ok(k - 3):
                    scale(k - 3)
                if ok(k - 1):
                    combine(k - 1)
                    sq(k - 1, "x")
                if ok(k - 3):
                    evac(k - 3)
            else:
                if ok(k - 2):
                    scale(k - 2)
                if ok(k - 2):
                    evac(k - 2)
                if ok(k):
                    r1(k, 2)
                    r2(k, 2)

    def inproj_hyena(hT, BhT, own, ctx_off):
        fw.mark("inproj_hyena")
        wsm = apool(6, [KC, 128], BF16, "wsm")
        zp = apool(2, [Lq + 2], F32, "z")
        zcp = apool(3, [Lq], F32, "zc")
        ustage = apool(2, [NTB, CG], BF16, "ustage")
        x0st = apool(2, [Lq], BF16, "x0st")
        ubf3 = apool(3, [Lq], BF16, "ubf3")
        pending = []

        def flush():
            while pending:
                ub, ubB, ust, uB, j = pending.pop(0)
                for tb0 in range(0, NTB, 4):
                    n4 = min(4, NTB - tb0)
                    pb, pB = psum_all.next()
                    pv = pb[:].bitcast(BF16)
                    transpose_group([pv[:, i * 128:(i + 1) * 128] for i in range(n4)],
                                    [ub[:, (tb0 + i) * 128:(tb0 + i + 1) * 128] for i in range(n4)], [ubB], [pB])
                    cp("act", ust[:, tb0:tb0 + n4, j * 128:(j + 1) * 128], pv[:, 0:n4 * 128].rearrange("p (a b) -> p a b", a=n4),
                       [pB], [uB], partial=True)

        whichs = [("x1", c.OX1), ("v", c.OV)] + ([("x0", c.OX0)] if own else [])

        def wl_h(key):
            ch_, off_ = key
            wb, wB = wsm.next()
            wload(wb[:], wB, W["w_in"], 0, KC, off_ + ch_ * 128, 128)
            return wb, wB
        wit = prefetched([(g_ * NCH + j_, off_) for g_ in range(NCG) for j_ in range(NCH) for _, off_ in whichs], wl_h, depth=3)
        for g in range(NCG):
            ust, uB = ustage.next()
            for j in range(NCH):
                ch = g * NCH + j
                res = {}
                for wi, (which, off) in enumerate(whichs):
                    gch = off // 128 + ch
                    _, (wb, wB) = next(wit)
                    z, zB = zp.next()
                    for tg in range(NTG):
                        pb, pB = psum_all.next()
                        mm_group(pb[:, 0:TT], [(wb[:, k, :], hT[:, k, 1 + tg * TT:1 + (tg + 1) * TT]) for k in range(KC)], [wB, BhT], [pB])
                        cp("act", z[:, 1 + tg * TT:1 + (tg + 1) * TT], pb[:, 0:TT], [pB], [zB], partial=True)
                    pb, pB = psum_all.next()
                    fw.tiny = True
                    mm_group(pb[:, 0:2], [(wb[:, k, :], hT[:, k, 0:Lq + 2:Lq + 1]) for k in range(KC)], [wB, BhT], [pB])
                    fw.tiny = False
                    cp("act", z[:, 0:Lq + 2:Lq + 1], pb[:, 0:2], [pB], [zB], partial=True)
                    if wi == 0:
                        flush()
                    zc, zcB = zcp.next()
                    ts("dve", zc[:], z[:, 1:Lq + 1], cwT[:, 1, gch:gch + 1], cbT[:, gch:gch + 1], ALU.mult, ALU.add, [zB, B_const], [zcB])
                    stt("dve", zc[:], z[:, 0:Lq], cwT[:, 0, gch:gch + 1], zc[:], ALU.mult, ALU.add, [zB, B_const, zcB], [zcB])
                    stt("dve", zc[:], z[:, 2:Lq + 2], cwT[:, 2, gch:gch + 1], zc[:], ALU.mult, ALU.add, [zB, B_const, zcB], [zcB])
                    res[which] = (zc, zcB)
                ub, ubB = ubf3.next()
                tt("dve", ub[:], res["x1"][0][:], res["v"][0][:], ALU.mult, [res["x1"][1], res["v"][1]], [ubB])
                pending.append((ub, ubB, ust, uB, j))
                if own:
                    xs, xsB = x0st.next()
                    cp("act", xs[:], res["x0"][0][:], [res["x0"][1]], [xsB])
                    fw.dma(STQ, x0T_s[ch * 128:(ch + 1) * 128, :], xs[:], reads=[xsB], writes=[B_x0T], partial=True)
            flush()
            fw.dma(STQ, u_s[g, ctx_off:ctx_off + Lq, :].rearrange("(tb p) c -> p tb c", p=128), ust[:], reads=[uB], writes=[B_u], partial=True)

    def inproj_qkv(hT, BhT, own, sp, ctx_off):
        fw.mark("inproj_qkv")
        wp = apool(3, [KC, QG], BF16, "wqkv")
        ropet, Brt = aalloc([NTB, 16], F32, "ropet")
        fw.dma("sp", ropet[:], rope[sp].rearrange("(tb p) k -> p tb k", p=128), writes=[Brt])
        NS = QG // 64
        tmpp = apool(2, [QG], F32, "qtmp")
        rtmp = apool(2, [4, NS, 8], F32, "rtmp")
        qkb = apool(3, [QG], BF16, "qkb")
        stage = apool(2, [QG // 128, Lq], BF16, "qkstage")
        wpv = apool(NQG, [KC, QG], BF16, "wv")
        vstate = {}

        def emit_v_loads():
            vst = apool(3, [NH, 129], BF16, "vst")
            for vt_, vB in vst.items:
                fw.op("pool", lambda e, vt_=vt_: e.memset(vt_[:], 1.0), writes=[vB])
            wbs = []
            for cg in range(NQG):
                wb, wB = wpv.next()
                wload(wb[:], wB, W["w_in"], 0, KC, c.OVA + cg * QG, QG)
                wbs.append((wb, wB))
            vstate['vst'] = vst
            vstate['wbs'] = wbs

        def wl_qk(key):
            off_, cg_ = key
            wb, wB = wp.next()
            wload(wb[:], wB, W["w_in"], 0, KC, off_ + cg_ * QG, QG)
            return wb, wB
        qk_list = [(which_, off_, cg_) for which_, off_ in ([("q", c.OQ)] if own else []) + [("k", c.OK)] for cg_ in range(NQG)]
        wit = prefetched([(off_, cg_) for _, off_, cg_ in qk_list], wl_qk, depth=1)
        for which, off, cg in qk_list:
            if True:
                _, (wb, wB) = next(wit)
                if not vstate:
                    emit_v_loads()
                stg, sB = stage.next()
                n4 = QG // 128

                def front(tb, wb=wb, wB=wB):
                    pb, pB = psum_all.next()
                    mm_group(pb[:, 0:QG], [(hT[:, k, 1 + tb * 128:1 + (tb + 1) * 128], wb[:, k, :]) for k in range(KC)], [BhT, wB], [pB])
                    tm, tB = tmpp.next()
                    cp("act", tm[:], pb[:, 0:QG], [pB], [tB])
                    v3 = tm[:].rearrange("p (s d) -> p s d", d=64)
                    x1, x2 = v3[:, :, 0:8], v3[:, :, 8:16]
                    cos = ropet[:, tb, 0:8].unsqueeze(1).to_broadcast([128, NS, 8])
                    sin = ropet[:, tb, 8:16].unsqueeze(1).to_broadcast([128, NS, 8])
                    r, rB = rtmp.next()
                    tt("dve", r[:, 0], x1, cos, ALU.mult, [tB, Brt], [rB])
                    tt("dve", r[:, 1], x2, sin, ALU.mult, [tB, Brt], [rB])
                    tt("dve", r[:, 2], x2, cos, ALU.mult, [tB, Brt], [rB])
                    tt("dve", r[:, 3], x1, sin, ALU.mult, [tB, Brt], [rB])
                    tt("dve", x1, r[:, 0], r[:, 1], ALU.subtract, [rB, tB], [tB])
                    tt("dve", x2, r[:, 2], r[:, 3], ALU.add, [rB, tB], [tB])
                    qb, qB = qkb.next()
                    cp("dve", qb[:], tm[:], [tB], [qB])
                    return qb, qB

                def back(tb, qb, qB, stg=stg, sB=sB):
                    pt, ptB = psum_all.next()
                    pv = pt[:].bitcast(BF16)
                    transpose_group([pv[:, i * 128:(i + 1) * 128] for i in range(n4)], [qb[:, i * 128:(i + 1) * 128] for i in range(n4)], [qB], [ptB])
                    cp("act", stg[:, :, tb * 128:(tb + 1) * 128], pv[:, 0:n4 * 128].rearrange("p (a b) -> p a b", a=n4), [ptB], [sB], partial=True)

                prev = None
                for tb in range(NTB + 1):
                    cur = front(tb) if tb < NTB else None
                    if prev is not None:
                        back(tb - 1, *prev)
                    prev = cur
                if which == "q":
                    fw.dma(STQ, qT_s[cg * QG:(cg + 1) * QG, :].rearrange("(a p) t -> p a t", p=128), stg[:], reads=[sB], writes=[B_qT], partial=True)
                else:
                    fw.dma(STQ, kT_s[cg * QG:(cg + 1) * QG, ctx_off:ctx_off + Lq].rearrange("(a p) t -> p a t", p=128), stg[:],
                           reads=[sB], writes=[B_kT], partial=True)
        vst, wbs = vstate['vst'], vstate['wbs']
        for tb in range(NTB):
            vt_, vB = vst.next()
            for cg in range(NQG):
                wb, wB = wbs[cg]
                pb, pB = psum_all.next()
                mm_group(pb[:, 0:QG], [(hT[:, k, 1 + tb * 128:1 + (tb + 1) * 128], wb[:, k, :]) for k in range(KC)], [BhT, wB], [pB])
                nh = QG // 128
                cp("act", vt_[:, cg * nh:(cg + 1) * nh, 0:128], pb[:, 0:QG].rearrange("p (a b) -> p a b", a=nh), [pB], [vB], partial=True)
            fw.dma(STQ, v_s[ctx_off + tb * 128:ctx_off + (tb + 1) * 128, :, :], vt_[:], reads=[vB], writes=[B_v], partial=True)

    def inproj_gates(hT, BhT):
        fw.mark("inproj_gates")
        wg = apool(3, [KC, DG], BF16, "wgate")
        gst = apool(3, [Lq], F32, "gst")
        def wl_g(key):
            gi_, cg_ = key
            wb, wB = wg.next()
            wload(wb[:], wB, W["w_in"], 0, KC, c.OG0 + gi_ * D + cg_ * DG, DG)
            return wb, wB
        for (gi, cg), (wb, wB) in prefetched([(gi_, cg_) for gi_ in range(2) for cg_ in range(NDG)], wl_g, depth=1):
            if True:
                for j in range(DG // 128):
                    dc = cg * (DG // 128) + j
                    gt, gB = gst.next()
                    for tg in range(NTG):
                        pb, pB = psum_all.next()
                        mm_group(pb[:, 0:TT], [(wb[:, k, j * 128:(j + 1) * 128], hT[:, k, 1 + tg * TT:1 + (tg + 1) * TT]) for k in range(KC)], [wB, BhT], [pB])
                        act(gt[:, tg * TT:(tg + 1) * TT], pb[:, 0:TT], AF.Sigmoid, [pB, B_const], [gB], partial=True, bias=bgT[:, gi, dc:dc + 1])
                    fw.dma(STQ, gates_s[gi, dc * 128:(dc + 1) * 128, :], gt[:], reads=[gB], writes=[B_gates], partial=True)

    def attn_phase(Lc):
        fw.mark("attn_phase")
        areset()
        nkb = Lc // 128
        NQS = TT // 128
        kTt = apool(2, [Lc], BF16, "kTt")
        qzp = apool(2, [2, Lq], BF16, "qz")
        vtp = apool(2, [nkb, 129], BF16, "vtp")
        pTp = apool(4, [TT], BF16, "pT")
        osb = apool(2, [NQS, 128], F32, "osb")
        ob16 = apool(2, [NQS, 128], BF16, "ob16")
        stp = apool(3, [24], F32, "ast")
        asbp = apool(2, [4, 2, 129], F32, "asb")
        junk, Bjunk = aalloc([NQS, 128], BF16, "ajunk")
        ystage = apool(2, [Lq], BF16, "ystage")
        scores = Rot(list(zip(banks[4:7], bankB[4:7])))
        trb, trB = banks[7], bankB[7]
        for qz, qB in qzp.items:
            fw.op("pool", lambda e, qz=qz: e.memset(qz[:], 0.0), writes=[qB])

        def acc(i, qs):
            b = i * 2 + qs // 2
            o = (qs % 2) * 256
            return banks[b][:, o:o + 129], bankB[b]

        def load_head(h):
            kt, kB = kTt.next()
            fw.dma("sp", kt[:], kT_s[h * 128:(h + 1) * 128, 0:Lc], reads=[B_kT], writes=[kB])
            qz, qB = qzp.next()
            fw.dma("sp", qz[0:64, 0, :], qT_s[h * 128:h * 128 + 64, :], reads=[B_qT, qB], writes=[qB], partial=True)
            fw.dma("sp", qz[64:128, 1, :], qT_s[h * 128 + 64:(h + 1) * 128, :], reads=[B_qT, qB], writes=[qB], partial=True)
            vv, vB = vtp.next()
            fw.dma("sp", vv[:], v_s[0:Lc, h, :].rearrange("(kb p) e -> p kb e", p=128), reads=[B_v], writes=[vB])
            return kt, kB, qz, qB, vv, vB

        pending = []
        pending_a = []

        def flush_a():
            while pending_a:
                pending_a.pop(0)()

        def flush():
            flush_a()
            while pending:
                pending.pop(0)()

        nxt = load_head(0)
        for h in range(NH):
            kt, kB, qz, qB, vv, vB = nxt
            if h + 1 < NH:
                nxt = load_head(h + 1)
            ys, ysB = ystage.next()
            for qt in range(NTG):
                steps = [(kb, i) for kb in range(nkb) for i in range(2)]
                n = len(steps)

                def qk(s_):
                    kb, i = steps[s_]
                    sc, scB = scores.next()
                    mm_group(sc[:, 0:TT], [(kt[:, kb * 128:(kb + 1) * 128], qz[:, i, qt * TT:(qt + 1) * TT])], [kB, qB], [scB])
                    return sc, scB

                scq = [qk(0), qk(1)]
                for s_ in range(n):
                    kb, i = steps[s_]
                    sc, scB = scq.pop(0)
                    pT, pTB = pTp.next()
                    act(pT[:], sc[:, 0:TT], AF.Exp, [scB], [pTB], scale=0.125)
                    if s_ + 2 < n:
                        scq.append(qk(s_ + 2))
                    if s_ == min(6, n - 1):
                        flush_a()
                    if s_ == min(14, n - 1):
                        flush()
                    for qs in range(NQS):
                        a, aB = acc(i, qs)
                        fw.op("pe", lambda e, a=a, pT=pT, qs=qs, vv=vv, kb=kb: e.matmul(
                            a, lhsT=pT[:, qs * 128:(qs + 1) * 128], rhs=vv[:, kb, :], start=(kb == 0 and qs % 2 == 0),
                            stop=(kb == nkb - 1), skip_group_check=True),
                            reads=[pTB, vB], writes=[aB], inc=(qs == NQS - 1))
                s4, sB = stp.next()
                o, oB = osb.next()
                asb, asB = asbp.next()
                for b in sorted({i_ * 2 + qs_ // 2 for i_ in range(2) for qs_ in range(NQS)}):
                    cp("act", asb[:, b, :, :], banks[b][:, :].rearrange("p (t c) -> p t c", t=2)[:, :, 0:129], [bankB[b]], [asB], partial=True)

                def acc_sb(i_, qs_):
                    return asb[:, i_ * 2 + qs_ // 2, qs_ % 2, :], asB
                for qs in range(NQS):
                    a0, B0 = acc_sb(0, qs)
                    a1, B1 = acc_sb(1, qs)
                    fw.op("dve", lambda e, s4=s4, a0=a0, qs=qs: e.reciprocal(out=s4[:, qs:qs + 1], in_=a0[:, 128:129]), reads=[B0], writes=[sB], partial=True)
                    fw.op("dve", lambda e, s4=s4, a1=a1, qs=qs: e.reciprocal(out=s4[:, 4 + qs:5 + qs], in_=a1[:, 128:129]), reads=[B1], writes=[sB], partial=True)
                ts("dve", s4[:, 8:8 + NQS], s4[:, 4:4 + NQS], cst[:, 2:3], None, ALU.mult, None, [sB, B_const], [sB])
                for qs in range(NQS):
                    a0, B0 = acc_sb(0, qs)
                    a1, B1 = acc_sb(1, qs)
                    ts("dve", o[:, qs, :], a0[:, 0:128], s4[:, qs:qs + 1], None, ALU.mult, None, [B0, sB], [oB], partial=(qs > 0))
                    stt("dve", o[:, qs, :], a1[:, 0:128], s4[:, 8 + qs:9 + qs], o[:, qs, :], ALU.mult, ALU.add, [B1, sB, oB], [oB], partial=True)
                def mid(o=o, oB=oB, s4=s4, sB=sB):
                    for qs in range(NQS):
                        act(junk[:, qs, :], o[:, qs, :], AF.Square, [oB], [Bjunk, sB], partial=(qs > 0), accum_out=s4[:, 12 + qs:13 + qs])
                    act(s4[:, 16:16 + NQS], s4[:, 12:12 + NQS], AF.Ln, [sB, B_const], [sB], scale=1.0 / 128, bias=cst[:, 3:4])
                    act(s4[:, 16:16 + NQS], s4[:, 16:16 + NQS], AF.Exp, [sB], [sB], scale=-0.5)
                    ob, obB = ob16.next()
                    for qs in range(NQS):
                        stt("dve", ob[:, qs, :], o[:, qs, :], s4[:, 16 + qs:17 + qs], gsub[:], ALU.mult, ALU.mult, [oB, sB, B_const], [obB], partial=(qs > 0))
                    return ob, obB

                def back(mid=mid, ys=ys, ysB=ysB, qt=qt):
                    ob, obB = mid.result
                    pv = trb[:].bitcast(BF16)
                    transpose_group([pv[:, qs * 128:(qs + 1) * 128] for qs in range(NQS)], [ob[:, qs, :] for qs in range(NQS)], [obB], [trB])
                    cp("act", ys[:, qt * TT:(qt + 1) * TT], pv[:, 0:TT], [trB], [ysB], partial=True)

                def run_mid(mid=mid):
                    mid.result = mid()
                pending_a.append(run_mid)
                pending.append(back)
            pending.append(lambda h=h, ys=ys, ysB=ysB: fw.dma(STQ, yattT_s[h * 128:(h + 1) * 128, :], ys[:], reads=[ysB], writes=[B_yattT], partial=True))
        flush_a()
        flush()

    def hyena_phase(Lc, Ft_d, Gt_d, kf_d, B_kf):
        fw.mark("hyena_phase")
        N = 2 * Lc
        nt, nf = Lc // 128, N // 128
        half = nf // 2
        for g in range(NCG):
            areset()
            Yf, BYf = aalloc([nf, CG], BF16, "Yf")
            mark = ar["off"]
            u_sb, Bu = aalloc([nt, CG], BF16, "u_sb")
            fw.dma("sp", u_sb[:], u_s[g, 0:Lc, :].rearrange("(tb p) c -> p tb c", p=128), reads=[B_u], writes=[Bu])
            fpool = apool(4, [nt, 128], BF16, "hfblk")
            kp = apool(4, [CG], F32, "hkf")
            tp = apool(8, [CG], F32, "htmp")
            for jp in range(half):
                fre, fB1 = fpool.next()
                fw.dma("sp", fre[:], Ft_d[jp], writes=[fB1])
                fim, fB2 = fpool.next()
                fw.dma("sp", fim[:], Ft_d[half + jp], writes=[fB2])
                pre, pB1 = psum_all.next()
                mm_group(pre[:, 0:CG], [(fre[:, tc, :], u_sb[:, tc, :]) for tc in range(nt)], [fB1, Bu], [pB1])
                pim, pB2 = psum_all.next()
                mm_group(pim[:, 0:CG], [(fim[:, tc, :], u_sb[:, tc, :]) for tc in range(nt)], [fB2, Bu], [pB2])
                kre, kB1 = kp.next()
                fw.dma("sp", kre[:], kf_d[0, jp * 128:(jp + 1) * 128, g * CG:(g + 1) * CG], reads=[B_kf], writes=[kB1])
                kim, kB2 = kp.next()
                fw.dma("sp", kim[:], kf_d[1, jp * 128:(jp + 1) * 128, g * CG:(g + 1) * CG], reads=[B_kf], writes=[kB2])
                (t1, tB1), (t2, tB2), (t3, tB3), (t4, tB4) = tp.next(), tp.next(), tp.next(), tp.next()
                tt("dve", t1[:], pre[:, 0:CG], kre[:], ALU.mult, [pB1, kB1], [tB1])
                tt("dve", t2[:], pim[:, 0:CG], kim[:], ALU.mult, [pB2, kB2], [tB2])
                tt("dve", t3[:], pre[:, 0:CG], kim[:], ALU.mult, [pB1, kB2], [tB3])
                tt("dve", t4[:], pim[:, 0:CG], kre[:], ALU.mult, [pB2, kB1], [tB4])
                tt("pool", Yf[:, jp, :], t1[:], t2[:], ALU.subtract, [tB1, tB2], [BYf], partial=True)
                tt("pool", Yf[:, half + jp, :], t3[:], t4[:], ALU.add, [tB3, tB4], [BYf], partial=True)
                if jp == 0:
                    cp("pool", Yf[0:1, 0, :], t1[0:1, :], [tB1, BYf], [BYf], partial=True)
                    cp("pool", Yf[0:1, half, :], t2[0:1, :], [tB2, BYf], [BYf], partial=True)
            if Lc > Lq:
                areset(mark)
            gpool = apool(2, [nf, 128], BF16, "hgblk")
            ybf = apool(2, [CG], BF16, "hybf")
            x0sb, Bx0 = aalloc([NCH, Lq], BF16, "x0sb")
            ystg, Bys = aalloc([NCH, Lq], BF16, "hystg")
            fw.dma("sp", x0sb[:], x0T_s[g * CG:(g + 1) * CG, :].rearrange("(j p) t -> p j t", p=128), reads=[B_x0T], writes=[Bx0])
            for tb in range(NTB):
                gb_, gB = gpool.next()
                fw.dma("sp", gb_[:], Gt_d[tb], writes=[gB])
                py, pyB = psum_all.next()
                mm_group(py[:, 0:CG], [(gb_[:, fc, :], Yf[:, fc, :]) for fc in range(nf)], [gB, BYf], [pyB])
                yb, ybB = ybf.next()
                cp("act", yb[:], py[:, 0:CG], [pyB], [ybB])
                ptr, ptB = psum_all.next()
                pv = ptr[:].bitcast(BF16)
                transpose_group([pv[:, j * 128:(j + 1) * 128] for j in range(NCH)], [yb[:, j * 128:(j + 1) * 128] for j in range(NCH)], [ybB], [ptB])
                tt("dve", ystg[:, :, tb * 128:(tb + 1) * 128], pv[:, 0:NCH * 128].rearrange("p (a b) -> p a b", a=NCH),
                   x0sb[:, :, tb * 128:(tb + 1) * 128], ALU.mult, [ptB, Bx0], [Bys], partial=True)
            fw.dma(STQ, yhyT_s[g * CG:(g + 1) * CG, :].rearrange("(j p) t -> p j t", p=128), ystg[:], reads=[Bys], writes=[B_yhyT], partial=True)
    def mix_phase():
        fw.mark("mix_phase")
        areset()
        yh, _ = aalloc([HC, Lq], BF16, "yh")
        ya, _ = aalloc([AC, Lq], BF16, "ya")
        Byhs = [fw.newbuf("yhtg") for _ in range(NTG)]
        Byas = [fw.newbuf("yatg") for _ in range(NTG)]
        for tg in range(NTG):
            tsl = slice(tg * TT, (tg + 1) * TT)
            fw.dma("sp", yh[:, :, tsl], yhyT_s[:, tsl].rearrange("(c p) t -> p c t", p=128), reads=[B_yhyT], writes=[Byhs[tg]])
            fw.dma("sp", ya[:, :, tsl], yattT_s[:, tsl].rearrange("(c p) t -> p c t", p=128), reads=[B_yattT], writes=[Byas[tg]])
        wh = apool(2, [HC, DG], BF16, "wh")
        wa = apool(2, [AC, DG], BF16, "wa")
        gp = apool(4, [Lq], F32, "gp")
        t1p = apool(2, [TT], F32, "mt1")
        t2p = apool(2, [TT], F32, "mt2")
        mst = apool(2, [Lq], BF16, "mst")
        def wl_m(cg_):
            whb, whB = wh.next()
            wload(whb[:], whB, W["w_hy_out"], 0, HC, cg_ * DG, DG)
            wab, waB = wa.next()
            wload(wab[:], waB, W["w_att_out"], 0, AC, cg_ * DG, DG)
            return whb, whB, wab, waB
        for cg, (whb, whB, wab, waB) in prefetched(range(NDG), wl_m, depth=1):
            for j in range(DG // 128):
                dc = cg * (DG // 128) + j
                g0, gB0 = gp.next()
                fw.dma("sp", g0[:], gates_s[0, dc * 128:(dc + 1) * 128, :], reads=[B_gates], writes=[gB0])
                g1, gB1 = gp.next()
                fw.dma("sp", g1[:], gates_s[1, dc * 128:(dc + 1) * 128, :], reads=[B_gates], writes=[gB1])
                ms, msB = mst.next()
                for tg in range(NTG):
                    sl = slice(tg * TT, (tg + 1) * TT)
                    pa, paB = psum_all.next()
                    mm_group(pa[:, 0:TT], [(whb[:, k, j * 128:(j + 1) * 128], yh[:, k, sl]) for k in range(HC)], [whB, Byhs[tg]], [paB])
                    pb, pbB = psum_all.next()
                    mm_group(pb[:, 0:TT], [(wab[:, k, j * 128:(j + 1) * 128], ya[:, k, sl]) for k in range(AC)], [waB, Byas[tg]], [pbB])
                    t1, tB1 = t1p.next()
                    tt("dve", t1[:], pa[:, 0:TT], g0[:, sl], ALU.mult, [paB, gB0], [tB1])
                    t2, tB2 = t2p.next()
                    tt("dve", t2[:], pb[:, 0:TT], g1[:, sl], ALU.mult, [pbB, gB1], [tB2])
                    tt("dve", ms[:, sl], t1[:], t2[:], ALU.add, [tB1, tB2], [msB], partial=True)
                fw.dma(STQ, mT_s[dc * 128:(dc + 1) * 128, :], ms[:], reads=[msB], writes=[B_mT], partial=True)

    def outproj_phase():
        fw.mark("outproj_phase")
        areset()
        mT, _ = aalloc([KC, Lq], BF16, "mT")
        BmTs = [fw.newbuf("mTtg") for _ in range(NTG)]
        for tg in range(NTG):
            fw.dma("sp", mT[:, :, tg * TT:(tg + 1) * TT], mT_s[:, tg * TT:(tg + 1) * TT].rearrange("(c p) t -> p c t", p=128), reads=[B_mT], writes=[BmTs[tg]])
        wp = apool(3, [KC, DG], BF16, "wout")
        stg = apool(3, [DG], F32, "ostg")
        def wl_o(cg_):
            wb, wB = wp.next()
            wload(wb[:], wB, W["w_out"], 0, KC, cg_ * DG, DG)
            return wb, wB
        for cg, (wb, wB) in prefetched(range(NDG), wl_o, depth=2):
            for tb in range(NTB):
                pb, pB = psum_all.next()
                mm_group(pb[:, 0:DG], [(mT[:, k, tb * 128:(tb + 1) * 128], wb[:, k, :]) for k in range(KC)], [BmTs[tb * 128 // TT], wB], [pB])
                st, sB = stg.next()
                cp("act" if tb % 2 else "dve", st[:], pb[:, 0:DG], [pB], [sB])
                fw.dma(STQ, br_s[tb * 128:(tb + 1) * 128, cg * DG:(cg + 1) * DG], st[:], reads=[sB], writes=[B_br], partial=True)

    FG = 512 if DFF % 512 == 0 else (256 if DFF % 256 == 0 else 128)
    NFG = DFF // FG
    FP = next(p for p in (11, 16, 8, 6, 4, 3, 2, 1) if FC % p == 0)
    NFP = FC // FP

    def ffn_up_phase(h2T, Bh2):
        fw.mark("ffn_up_phase")
        wpg = apool(2, [KC, FG], BF16, "wfg")
        wpu = apool(2, [KC, FG], BF16, "wfu")
        sil = apool(2, [TT], F32, "sil")
        ast = apool(2, [Lq], BF16, "ast")
        def wl_f(fg_):
            wg_, wgB = wpg.next()
            wload(wg_[:], wgB, W["w_ffn_in"], 0, KC, fg_ * FG, FG)
            wu_, wuB = wpu.next()
            wload(wu_[:], wuB, W["w_ffn_in"], 0, KC, DFF + fg_ * FG, FG)
            return wg_, wgB, wu_, wuB
        for fg, (wg_, wgB, wu_, wuB) in prefetched(range(NFG), wl_f, depth=1):
            for j in range(FG // 128):
                fc = fg * (FG // 128) + j
                as_, aB = ast.next()
                for tg in range(NTG):
                    sl = slice(tg * TT, (tg + 1) * TT)
                    pg, pgB = psum_all.next()
                    mm_group(pg[:, 0:TT], [(wg_[:, k, j * 128:(j + 1) * 128], h2T[:, k, sl]) for k in range(KC)], [wgB, Bh2], [pgB])
                    pu, puB = psum_all.next()
                    mm_group(pu[:, 0:TT], [(wu_[:, k, j * 128:(j + 1) * 128], h2T[:, k, sl]) for k in range(KC)], [wuB, Bh2], [puB])
                    s, sB = sil.next()
                    act(s[:], pg[:, 0:TT], AF.Silu, [pgB], [sB])
                    tt("dve", as_[:, sl], pu[:, 0:TT], s[:], ALU.mult, [puB, sB], [aB], partial=True)
                fw.dma(STQ, aT_s[fc * 128:(fc + 1) * 128, :], as_[:], reads=[aB], writes=[B_aT], partial=True)

    def ffn_down_phase():
        fw.mark("ffn_down_phase")
        areset()
        atp = apool(2, [FC, TT], BF16, "aTt")
        wp = apool(4, [FP, DG], BF16, "wdown")
        stg = apool(3, [DG], F32, "dstg")
        NQS = TT // 128
        def wl_d(key):
            _, cg_, pc_ = key
            wb, wB = wp.next()
            src = wfo16[pc_ * FP * 128:(pc_ + 1) * FP * 128, cg_ * DG:(cg_ + 1) * DG].rearrange("(c p) n -> p c n", p=128)
            fw.dma("sp", wb[:], src, reads=[B_wfo], writes=[wB])
            return wb, wB
        wit = prefetched([(tg_, cg_, pc_) for tg_ in range(NTG) for cg_ in range(NDG) for pc_ in range(NFP)], wl_d, depth=2)

        def al_d(tg_):
            at, aB = atp.next()
            fw.dma("sp", at[:], aT_s[:, tg_ * TT:(tg_ + 1) * TT].rearrange("(k p) t -> p k t", p=128), reads=[B_aT], writes=[aB])
            return at, aB
        for tg, (at, aB) in prefetched(range(NTG), al_d, depth=1):
            for cg in range(NDG):
                accs = [psum_all.next() for _ in range(NQS)]
                for pc in range(NFP):
                    _, (wb, wB) = next(wit)
                    for qs in range(NQS):
                        pa, paB = accs[qs]
                        for kk in range(FP):
                            k = pc * FP + kk
                            fw.op("pe", lambda e, pa=pa, at=at, k=k, qs=qs, wb=wb, kk=kk: e.matmul(
                                pa[:, 0:DG], lhsT=at[:, k, qs * 128:(qs + 1) * 128], rhs=wb[:, kk, :], start=(k == 0), stop=(k == FC - 1)),
                                reads=[aB, wB], writes=[paB], inc=(kk == FP - 1))
                for qs in range(NQS):
                    pa, paB = accs[qs]
                    st, sB = stg.next()
                    cp("act" if qs % 2 else "dve", st[:], pa[:, 0:DG], [paB], [sB])
                    r0 = tg * TT + qs * 128
                    fw.dma(STQ, br_s[r0:r0 + 128, cg * DG:(cg + 1) * DG], st[:], reads=[sB], writes=[B_br], partial=True)

    def ple_phase(seg, h3T, Bh3, pT, BpT):
        fw.mark("ple_phase")
        wpg = apool(2, [KC, DG], BF16, "wpleg")
        wpi = apool(2, [PC, DG], BF16, "wplei")
        x2t = apool(3, [DG], F32, "x2t")
        et = apool(2, [DG], F32, "et")
        ot = apool(3, [DG], F32, "ot")
        def wl_p(cg_):
            wg_, wgB = wpg.next()
            wload(wg_[:], wgB, W["w_ple_gate"], 0, KC, cg_ * DG, DG)
            wi_, wiB = wpi.next()
            wload(wi_[:], wiB, W["w_ple_in"], 0, PC, cg_ * DG, DG)
            return wg_, wgB, wi_, wiB
        for cg, (wg_, wgB, wi_, wiB) in prefetched(range(NDG), wl_p, depth=1):
            for tb in range(NTB):
                tsl = slice(tb * 128, (tb + 1) * 128)
                pe_, peB = psum_all.next()
                mm_group(pe_[:, 0:DG], [(h3T[:, k, tsl], wg_[:, k, :]) for k in range(KC)], [Bh3, wgB], [peB])
                pp, ppB = psum_all.next()
                mm_group(pp[:, 0:DG], [(pT[:, k, tsl], wi_[:, k, :]) for k in range(PC)], [BpT, wiB], [ppB])
                e_, eB = et.next()
                act(e_[:], pe_[:, 0:DG], AF.Sigmoid, [peB], [eB])
                x2, xB = x2t.next()
                fw.dma("sp", x2[:], x2_s[tsl, cg * DG:(cg + 1) * DG], reads=[B_x2], writes=[xB])
                o, oB = ot.next()
                tt("dve", o[:], pp[:, 0:DG], e_[:], ALU.mult, [ppB, eB], [oB])
                tt("dve", o[:], o[:], x2[:], ALU.add, [oB, xB], [oB])
                fw.dma(STQ, yq[seg, tsl, cg * DG:(cg + 1) * DG], o[:], reads=[oB], writes=[B_out], partial=True)

    for r0 in range(0, DFF, 512):
        r1 = min(DFF, r0 + 512)
        fw.dma("pool", wfo16[r0:r1, :], W["w_ffn_out"][r0:r1, :], writes=[B_wfo], partial=True)
    filter_phase(LP, embP, decP, e0P, FtP, kfP, B_kfP)
    filter_phase(LS, embS, decS, e0S, FtS, kfS, B_kfS)
    for seg in range(3):
        subs = [(seg, xq[seg], True, 0)]
        if seg == 2:
            subs.append((3, xo, False, Lq))
        for sp, x_d, own, ctx_off in subs:
            areset()
            hT, BhT = aalloc([KC, Lq + 2], BF16, "hT")
            mark = ar["off"]
            norm_phase(x_d, None, None, None, None, None, 0, hT, BhT, 1, halo_d=xhalo[sp])
            areset(mark)
            inproj_hyena(hT, BhT, own, ctx_off)
            areset(mark)
            inproj_qkv(hT, BhT, own, sp, ctx_off)
            if own:
                areset(mark)
                inproj_gates(hT, BhT)
        Lc = 2 * Lq if seg == 2 else Lq
        if seg == 2:
            hyena_phase(Lc, FtS, GtS, kfS, B_kfS)
        else:
            hyena_phase(Lc, FtP, GtP, kfP, B_kfP)
        attn_phase(Lc)
        mix_phase()
        outproj_phase()
        areset()
        h2T, Bh2 = aalloc([KC, Lq], BF16, "h2T")
        mark = ar["off"]
        norm_phase(xq[seg], br_s, B_br, "g_mix_post", x1_s, B_x1, 1, h2T, Bh2, 0)
        areset(mark)
        ffn_up_phase(h2T, Bh2)
        ffn_down_phase()
        areset()
        h3T, Bh3 = aalloc([KC, Lq], BF16, "h3T")
        pT, BpT = aalloc([PC, Lq], BF16, "pT")
        mark = ar["off"]
        norm_phase(x1_s, br_s, B_br, "g_ffn_post", x2_s, B_x2, 2, h3T, Bh3, 0, p_d=pq[seg], pT=pT, BpT=BpT, B_x=B_x1)
        areset(mark)
        ple_phase(seg, h3T, Bh3, pT, BpT)
    fw.finish()
    fw.mark("end")
    nc._fw_stats = fw.stats
    nc._fw_marks = fw.marks
    return nc


def make_in_maps(cfg, inputs):
    c = cfg
    Lq, nco = c.Lq, c.ncores
    f32 = np.float32
    xp = np.asarray(inputs["x_prompt"], f32)
    xs = np.asarray(inputs["x_sample"], f32)
    pp = np.asarray(inputs["p_prompt"], f32)[0]
    ps = np.asarray(inputs["p_sample"], f32)[0]
    wmap = {}
    for k, v in inputs.items():
        if k in ("x_prompt", "x_sample", "p_prompt", "p_sample"):
            continue
        a = np.asarray(v, f32)[0]
        if a.ndim == 1:
            a = a[None, :]
        wmap[k] = np.ascontiguousarray(a)
    ordP = np.arange(c.LP)
    FtP, GtP = dft_tables(ordP, ordP)
    embP, decP, e0P = filter_tables(c.LP, ordP, c.HY)
    per_half = {}
    for hf in range(2):
        own = np.arange(hf * Lq, (hf + 1) * Lq)
        oth = np.arange((1 - hf) * Lq, (2 - hf) * Lq)
        order = np.concatenate([own, oth])
        FtS, GtS = dft_tables(order, own)
        embS, decS, e0S = filter_tables(c.LS, order, c.HY)
        per_half[hf] = (FtS, GtS, embS, decS, e0S, own, oth)
    in_maps = []
    for core in range(nco):
        hf = core % 2
        sq = core // 2
        FtS, GtS, embS, decS, e0S, own, oth = per_half[hf]
        m = dict(wmap)
        m["xq"] = np.ascontiguousarray(np.stack([xp[2 * core], xp[2 * core + 1], xs[sq, own]]))
        m["xo"] = np.ascontiguousarray(xs[sq, oth])
        halo = np.zeros((4, 2, c.D), f32)
        if hf == 0:
            halo[2, 1] = xs[sq, Lq]
            halo[3, 0] = xs[sq, Lq - 1]
        else:
            halo[2, 0] = xs[sq, Lq - 1]
            halo[3, 1] = xs[sq, Lq]
        m["xhalo"] = halo
        m["pq"] = np.ascontiguousarray(np.stack([pp[2 * core], pp[2 * core + 1], ps[sq, own]]))
        m["rope"] = np.ascontiguousarray(np.stack([rope_table(ordP), rope_table(ordP), rope_table(own), rope_table(oth)]))
        m.update(FtP=FtP, GtP=GtP, FtS=FtS, GtS=GtS, embP=embP, embS=embS, decP=decP, decS=decS, e0P=e0P, e0S=e0S)
        in_maps.append(m)
    return in_maps


def assemble(cfg, results, n_prompt, n_sample):
    c = cfg
    Lq = c.Lq
    yp = np.empty((n_prompt, Lq, c.D), np.float32)
    ys = np.empty((n_sample, 2 * Lq, c.D), np.float32)
    for core, r in enumerate(results):
        y = np.asarray(r["yq"], np.float32)
        yp[2 * core] = y[0]
        yp[2 * core + 1] = y[1]
        hf = core % 2
        ys[core // 2, hf * Lq:(hf + 1) * Lq] = y[2]
    return yp, ys


_NC_CACHE = {}


def kernel(**inputs):
    cfg = FULL
    if "full" not in _NC_CACHE:
        _NC_CACHE["full"] = build(cfg)
    nc = _NC_CACHE["full"]
    in_maps = make_in_maps(cfg, inputs)
    res = run_bass_kernel_spmd(nc, in_maps, core_ids=list(range(cfg.ncores)))
    return assemble(cfg, res.results, 2 * cfg.ncores, cfg.ncores // 2)
```

```python
import math
import numpy as np
import ml_dtypes
import concourse.bass as bass
import concourse.mybir as mybir
from concourse.bass_utils import run_bass_kernel_spmd

F32 = mybir.dt.float32
BF16 = mybir.dt.bfloat16
AF = mybir.ActivationFunctionType
ALU = mybir.AluOpType
AX = mybir.AxisListType

EPOCH = 30000
NORM_EPS = 1e-6
TWO_PI = 2.0 * math.pi


class Buf:
    __slots__ = ("w", "r", "name")

    def __init__(self, name="", r=None):
        self.w = {}
        self.r = dict(r) if r else {}
        self.name = name


class Lane:
    def __init__(self, fw, name, step):
        self.fw = fw
        self.name = name
        self.step = step
        self.count = 0
        self.sems = []

    def sem_for(self, seq):
        ep = EPOCH // self.step
        e = (seq - 1) // ep
        while len(self.sems) <= e:
            self.sems.append(self.fw.nc.alloc_semaphore(f"s_{self.name}_{len(self.sems)}"))
        return self.sems[e], ((seq - 1) % ep + 1) * self.step


class Eng:
    def __init__(self, fw, name, is_compute):
        self.name = name
        self.instrs = []
        self.waited = {}
        self.lane = Lane(fw, name, 1) if is_compute else None
        self.dlanes = []
        self.dnext = 0


class FW:
    def __init__(self, nc, n_dma_lanes=8):
        self.nc = nc
        self.E = {
            "pe": Eng(self, "pe", True),
            "act": Eng(self, "act", True),
            "dve": Eng(self, "dve", True),
            "pool": Eng(self, "pool", True),
            "sp": Eng(self, "sp", False),
        }
        for q in ("sp", "pool"):
            self.E[q].dlanes = [Lane(self, f"d{q}{i}", 16) for i in range(n_dma_lanes)]
        self.snap = {}
        self.n_instr = 0
        self.marks = []
        self.pe_ops = 0
        self.tiny = False

    def all_lanes(self):
        for e in self.E.values():
            if e.lane is not None:
                yield e.lane
            for l in e.dlanes:
                yield l

    def mark(self, name):
        self.marks.append((name, self.pe_ops))

    def snapshot(self):
        self.snap = {l: l.count for l in self.all_lanes() if l.count > 0}

    def newbuf(self, name=""):
        return Buf(name, self.snap)

    def _need(self, eng, lane, seq, waits):
        if lane is eng.lane and lane.name == "pe":
            return
        if eng.waited.get(lane, 0) >= seq:
            return
        if waits.get(lane, 0) < seq:
            waits[lane] = seq

    def _deps(self, eng, reads, writes, partial):
        waits = {}
        for b in reads:
            for ln, sq in b.w.items():
                self._need(eng, ln, sq, waits)
        for b in writes:
            if not partial:
                for ln, sq in b.w.items():
                    self._need(eng, ln, sq, waits)
            for ln, sq in b.r.items():
                self._need(eng, ln, sq, waits)
        for ln, sq in waits.items():
            eng.waited[ln] = sq
            sem, val = ln.sem_for(sq)
            eng.instrs.append(("wait", sem, val))

    def _mark(self, lane, seq, reads, writes, partial):
        for b in reads:
            if b.r.get(lane, 0) < seq:
                b.r[lane] = seq
        for b in writes:
            if partial:
                b.w[lane] = seq
            else:
                b.w = {lane: seq}
                b.r = {}

    def op(self, ename, fn, reads=(), writes=(), inc=True, partial=False, pwrites=()):
        eng = self.E[ename]
        if ename == 'pe' and not self.tiny:
            self.pe_ops += 1
        if pwrites:
            self._deps(eng, (), pwrites, True)
        self._deps(eng, reads, writes, partial)
        lane = eng.lane
        seq = lane.count + 1
        if inc:
            lane.count = seq
            sem, _ = lane.sem_for(seq)
            eng.instrs.append(("op", fn, sem, 1))
        else:
            eng.instrs.append(("op", fn, None, 0))
        self._mark(lane, seq, reads, writes, partial)
        if pwrites:
            self._mark(lane, seq, (), pwrites, True)
        self.n_instr += 1

    def dma(self, qname, out, in_, reads=(), writes=(), partial=False, slow=False):
        eng = self.E[qname]
        lane = eng.dlanes[eng.dnext]
        eng.dnext = (eng.dnext + 1) % len(eng.dlanes)
        if lane.count > 0 and eng.waited.get(lane, 0) < lane.count:
            eng.waited[lane] = lane.count
            sem, val = lane.sem_for(lane.count)
            eng.instrs.append(("wait", sem, val))
        self._deps(eng, reads, writes, partial)
        seq = lane.count + 1
        lane.count = seq
        sem, _ = lane.sem_for(seq)
        if slow:
            eng.instrs.append(("op", lambda e, o=out, i=in_: e.dma_start(out=o, in_=i, allow_slow_non_contiguous=True), sem, 16))
        else:
            eng.instrs.append(("op", lambda e, o=out, i=in_: e.dma_start(out=o, in_=i), sem, 16))
        self._mark(lane, seq, reads, writes, partial)
        self.n_instr += 1

    def finish(self):
        self.stats = {n: (len(e.instrs), e.lane.count if e.lane else 0, [l.count for l in e.dlanes]) for n, e in self.E.items()}
        sp = self.E["sp"]
        for l in self.all_lanes():
            if l.count > 0 and sp.waited.get(l, 0) < l.count:
                sem, val = l.sem_for(l.count)
                sp.instrs.append(("wait", sem, val))
        nc = self.nc
        with nc.Block() as block:
            def replay(ename):
                def run(e):
                    for it in self.E[ename].instrs:
                        if it[0] == "wait":
                            e.wait_ge(it[1], it[2])
                        else:
                            ins = it[1](e)
                            if it[2] is not None:
                                ins.then_inc(it[2], it[3])
                return run
            block.tensor(replay("pe"))
            block.scalar(replay("act"))
            block.vector(replay("dve"))
            block.gpsimd(replay("pool"))
            block.sync(replay("sp"))


class Rot:
    def __init__(self, items):
        self.items = items
        self.i = 0

    def next(self):
        it = self.items[self.i]
        self.i = (self.i + 1) % len(self.items)
        return it


class Cfg:
    def __init__(self, D=2048, HY=1024, ATT=1024, DFF=5632, PLE=256, Lq=2048, ncores=8,
                 FE=33, FO=64, lam_init=0.2):
        self.D, self.HY, self.ATT, self.DFF, self.PLE, self.Lq = D, HY, ATT, DFF, PLE, Lq
        self.ncores = ncores
        self.FE, self.FO = FE, FO
        self.lam_init = lam_init
        self.KC, self.HC, self.AC, self.FC, self.PC = D // 128, HY // 128, ATT // 128, DFF // 128, PLE // 128
        self.NH = ATT // 128
        self.TT = min(512, Lq)
        self.NTG = Lq // self.TT
        self.NTB = Lq // 128
        self.CG = min(512, HY)
        self.NCG = HY // self.CG
        self.NCH = self.CG // 128
        self.QG = min(512, ATT)
        self.NQG = ATT // self.QG
        self.DG = min(512, D)
        self.NDG = D // self.DG
        self.LP = Lq
        self.LS = 2 * Lq
        self.IN_COLS = 3 * HY + 3 * ATT + 2 * D
        self.OX0, self.OX1, self.OV = 0, HY, 2 * HY
        self.OQ, self.OK, self.OVA = 3 * HY, 3 * HY + ATT, 3 * HY + 2 * ATT
        self.OG0 = 3 * HY + 3 * ATT
        self.OG1 = self.OG0 + D


FULL = Cfg()


def _bf(a):
    return np.ascontiguousarray(a.astype(ml_dtypes.bfloat16))


def dft_tables(order, tout):
    Lc = len(order)
    N = 2 * Lc
    half = N // 2
    t = order.astype(np.int64)[:, None]
    j = np.arange(half, dtype=np.int64)[None, :]
    ang = TWO_PI * ((t * j) % N).astype(np.float64) / N
    F = np.empty((Lc, N), np.float64)
    F[:, :half] = np.cos(ang)
    F[:, half:] = -np.sin(ang)
    F[:, half] = np.where(order % 2 == 0, 1.0, -1.0)
    to = tout.astype(np.int64)[None, :]
    jj = np.arange(half, dtype=np.int64)[:, None]
    ang2 = TWO_PI * ((jj * to) % N).astype(np.float64) / N
    G = np.empty((N, len(tout)), np.float64)
    G[:half] = 2.0 * np.cos(ang2) / N
    G[0] = 1.0 / N
    G[half:] = -2.0 * np.sin(ang2) / N
    G[half] = np.where(tout % 2 == 0, 1.0, -1.0) / N
    nt, nf, ntb = Lc // 128, N // 128, len(tout) // 128
    Ft = F.reshape(nt, 128, nf, 128).transpose(2, 1, 0, 3)
    Gt = G.reshape(nf, 128, ntb, 128).transpose(2, 1, 0, 3)
    return _bf(Ft), _bf(Gt)


def filter_tables(L, order, HY, FB=16):
    f32 = np.float32
    t_all = np.linspace(0.0, 1.0, L, dtype=f32)
    t = t_all[order]
    wpos = (f32(TWO_PI) * order.astype(f32) / f32(L)).astype(f32)
    bands = np.linspace(1e-4, FB - 1, FB, dtype=f32)
    ang = wpos[:, None] * bands[None, :]
    emb = np.concatenate([t[:, None], np.cos(ang), -np.sin(ang)], axis=-1).astype(f32)
    max_decay = math.log(1e-2) / 0.3
    min_decay = math.log(1e-2) / 1.5
    deltas = np.abs(np.linspace(min_decay, max_decay, HY, dtype=f32))
    dec = np.exp(-t[:, None] * deltas[None, :]).astype(f32)
    e0 = (order == 0).astype(f32)[None, :]
    return np.ascontiguousarray(emb.T), np.ascontiguousarray(dec.T), np.ascontiguousarray(e0)


def rope_table(pos, RD=16, theta=500000.0):
    inv = theta ** (-np.arange(0, RD, 2, dtype=np.float32) / RD)
    ang = pos.astype(np.float32)[:, None] * inv[None, :]
    return np.concatenate([np.cos(ang), np.sin(ang)], axis=-1).astype(np.float32)


def build(cfg, debug=False):
    c = cfg
    D, HY, ATT, DFF, PLE, Lq = c.D, c.HY, c.ATT, c.DFF, c.PLE, c.Lq
    KC, HC, AC, FC, PC, NH = c.KC, c.HC, c.AC, c.FC, c.PC, c.NH
    TT, NTG, NTB, CG, NCG, NCH = c.TT, c.NTG, c.NTB, c.CG, c.NCG, c.NCH
    QG, NQG, DG, NDG = c.QG, c.NQG, c.DG, c.NDG
    LP, LS = c.LP, c.LS
    nc = bass.Bass("TRN2", target_bir_lowering=False)
    fw = FW(nc)
    skind = "ExternalOutput" if debug else "Internal"
    STQ = "pool"

    def din(name, shape, dt=F32):
        return nc.dram_tensor(name, list(shape), dt, kind="ExternalInput").ap()

    def dscr(name, shape, dt):
        return nc.dram_tensor(name, list(shape), dt, kind=skind).ap()

    xq = din("xq", [3, Lq, D])
    xo = din("xo", [Lq, D])
    xhalo = din("xhalo", [4, 2, D])
    pq = din("pq", [3, Lq, PLE])
    rope = din("rope", [4, Lq, 16])
    nfP, ntP, nfS, ntS = 2 * LP // 128, LP // 128, 2 * LS // 128, LS // 128
    FtP = din("FtP", [nfP, 128, ntP, 128], BF16)
    GtP = din("GtP", [NTB, 128, nfP, 128], BF16)
    FtS = din("FtS", [nfS, 128, ntS, 128], BF16)
    GtS = din("GtS", [NTB, 128, nfS, 128], BF16)
    embP = din("embP", [c.FE, LP])
    embS = din("embS", [c.FE, LS])
    decP = din("decP", [HY, LP])
    decS = din("decS", [HY, LS])
    e0P = din("e0P", [1, LP])
    e0S = din("e0S", [1, LS])
    W = {}
    for name, shape in [
        ("g_mix_pre", [1, D]), ("g_mix_post", [1, D]), ("g_ffn_pre", [1, D]), ("g_ffn_post", [1, D]),
        ("g_ple", [1, D]), ("w_in", [D, c.IN_COLS]), ("b_gate", [2, D]), ("conv_w", [3, 3 * HY]),
        ("conv_b", [1, 3 * HY]), ("filt_w1", [c.FE, c.FO]), ("filt_b1", [1, c.FO]), ("filt_w2", [c.FO, c.FO]),
        ("filt_b2", [1, c.FO]), ("filt_freq", [1, c.FO]), ("filt_w3", [c.FO, 2 * HY]), ("hyena_d", [1, HY]),
        ("lam_q1", [1, 64]), ("lam_k1", [1, 64]), ("lam_q2", [1, 64]), ("lam_k2", [1, 64]),
        ("g_subln", [1, 128]), ("w_hy_out", [HY, D]), ("w_att_out", [ATT, D]), ("w_out", [D, D]),
        ("w_ffn_in", [D, 2 * DFF]), ("w_ffn_out", [DFF, D]), ("w_ple_in", [PLE, D]), ("w_ple_gate", [D, D]),
    ]:
        W[name] = din(name, shape)
    yq = nc.dram_tensor("yq", [3, Lq, D], F32, kind="ExternalOutput").ap()

    kfP = dscr("kfP", [2, LP, HY], F32)
    kfS = dscr("kfS", [2, LS, HY], F32)
    x0T_s = dscr("x0T_s", [HY, Lq], BF16)
    u_s = dscr("u_s", [NCG, LS, CG], BF16)
    qT_s = dscr("qT_s", [ATT, Lq], BF16)
    kT_s = dscr("kT_s", [ATT, LS], BF16)
    v_s = dscr("v_s", [LS, NH, 129], BF16)
    gates_s = dscr("gates_s", [2, D, Lq], F32)
    yhyT_s = dscr("yhyT_s", [HY, Lq], BF16)
    yattT_s = dscr("yattT_s", [ATT, Lq], BF16)
    mT_s = dscr("mT_s", [D, Lq], BF16)
    br_s = dscr("br_s", [Lq, D], F32)
    x1_s = dscr("x1_s", [Lq, D], F32)
    x2_s = dscr("x2_s", [Lq, D], F32)
    aT_s = dscr("aT_s", [DFF, Lq], BF16)
    wfo16 = nc.dram_tensor("wfo16", [DFF, D], BF16).ap()

    def dbuf(name):
        return Buf(name)
    B_kfP, B_kfS = dbuf("kfP"), dbuf("kfS")
    B_x0T, B_u, B_qT, B_kT, B_v = dbuf("x0T"), dbuf("u"), dbuf("qT"), dbuf("kT"), dbuf("v")
    B_gates, B_yhyT, B_yattT, B_mT = dbuf("gates"), dbuf("yhyT"), dbuf("yattT"), dbuf("mT")
    B_br, B_x1, B_x2, B_aT, B_out = dbuf("br"), dbuf("x1"), dbuf("x2"), dbuf("aT"), dbuf("out")
    B_wfo = dbuf("wfo16")

    def sb(name, shape, dt=F32):
        return nc.alloc_sbuf_tensor(name, list(shape), dt)
    ident = sb("ident", [128, 128], BF16)
    identf = sb("identf", [128, 128])
    gcol = sb("gcol", [128, 3, KC])
    cwT = sb("cwT", [128, 3, 3 * HC])
    cbT = sb("cbT", [128, 3 * HC])
    bgT = sb("bgT", [128, 2, KC])
    dcol = sb("dcol", [128, HC])
    cst = sb("cst", [128, 8])
    gsub = sb("gsub", [128, 128])
    lamt = sb("lamt", [128, 4, 64])
    lams = sb("lams", [128, 4])
    B_const = Buf("const")
    B_ident = Buf("ident")

    banks = [nc.alloc_psum_tensor(f"pb{i}", [128, 512], F32) for i in range(8)]
    bankB = [Buf(f"pb{i}") for i in range(8)]
    psum_all = Rot(list(zip(banks, bankB)))

    AW = (nc.sbuf_bytes_remaining - 4096) // 4
    AW -= AW % 8
    arena_t = nc.alloc_sbuf_tensor("arena", [128, AW], F32)
    ar = {"off": 0}

    def areset(to=0):
        ar["off"] = to
        fw.snapshot()

    def aalloc(shape, dt=F32, name=""):
        n = int(np.prod(shape))
        words = (n * (2 if dt == BF16 else 4) + 3) // 4
        words += (-words) % 8
        off = ar["off"]
        assert off + words <= AW, f"arena overflow {name}: {off}+{words}>{AW}"
        ar["off"] = off + words
        ap = arena_t[:, off:off + words]
        if dt != F32:
            ap = ap.bitcast(dt)
        ap = ap[:, 0:n]
        if len(shape) == 2:
            ap = ap.rearrange("p (a b) -> p a b", a=shape[0])
        elif len(shape) == 3:
            ap = ap.rearrange("p (a b c) -> p a b c", a=shape[0], b=shape[1])
        return ap, fw.newbuf(name)

    def apool(n, shape, dt=F32, name=""):
        return Rot([aalloc(shape, dt, f"{name}{i}") for i in range(n)])

    def mm_group(out_ap, pairs, reads, writes):
        n = len(pairs)
        for i, (l, r) in enumerate(pairs):
            fw.op("pe", lambda e, l=l, r=r, i=i: e.matmul(out_ap, lhsT=l, rhs=r, start=(i == 0), stop=(i == n - 1)),
                  reads=reads, writes=writes, inc=(i == n - 1))

    def transpose_group(out_aps, in_aps, reads, writes):
        n = len(out_aps)
        for i in range(n):
            fw.op("pe", lambda e, o=out_aps[i], a=in_aps[i]: e.transpose(out=o, in_=a, identity=ident[:]),
                  reads=list(reads) + [B_ident], writes=writes, inc=(i == n - 1))

    def act(out, in_, func, reads, writes, partial=False, pwrites=(), **kw):
        fw.op("act", lambda e: e.activation(out=out, in_=in_, func=func, **kw), reads=reads, writes=writes, partial=partial, pwrites=pwrites)

    def tt(eng, out, in0, in1, op, reads, writes, partial=False):
        fw.op(eng, lambda e: e.tensor_tensor(out=out, in0=in0, in1=in1, op=op), reads=reads, writes=writes, partial=partial)

    def ts(eng, out, in0, s1, s2, op0, op1, reads, writes, partial=False):
        if op1 is None:
            fw.op(eng, lambda e: e.tensor_scalar(out=out, in0=in0, scalar1=s1, scalar2=None, op0=op0), reads=reads, writes=writes, partial=partial)
        else:
            fw.op(eng, lambda e: e.tensor_scalar(out=out, in0=in0, scalar1=s1, scalar2=s2, op0=op0, op1=op1), reads=reads, writes=writes, partial=partial)

    def stt(eng, out, in0, scalar, in1, op0, op1, reads, writes, partial=False):
        fw.op(eng, lambda e: e.scalar_tensor_tensor(out=out, in0=in0, scalar=scalar, in1=in1, op0=op0, op1=op1),
              reads=reads, writes=writes, partial=partial)

    def cp(eng, out, in_, reads, writes, partial=False):
        if eng == "act":
            act(out, in_, AF.Copy, reads, writes, partial)
        else:
            fw.op(eng, lambda e: e.tensor_copy(out=out, in_=in_), reads=reads, writes=writes, partial=partial)

    def wload(dst, dstB, w_ap, k0, nk, c0, ncols, part=False):
        src = w_ap[k0 * 128:(k0 + nk) * 128, c0:c0 + ncols].rearrange("(c p) n -> p c n", p=128)
        fw.dma("pool", dst, src, writes=[dstB], partial=part)

    def prefetched(keys, loader, depth=1):
        keys = list(keys)
        handles = {}
        nxt = 0
        for idx, k in enumerate(keys):
            while nxt < len(keys) and nxt <= idx + depth:
                handles[nxt] = loader(keys[nxt])
                nxt += 1
            yield k, handles.pop(idx)

    def rstd_from_ss(st, stB, col_ss, col_out, n):
        ts("dve", st[:, col_out:col_out + 1], st[:, col_ss:col_ss + 1], 1.0 / n, NORM_EPS, ALU.mult, ALU.add, [stB], [stB])
        act(st[:, col_out:col_out + 1], st[:, col_out:col_out + 1], AF.Sqrt, [stB], [stB])
        fw.op("dve", lambda e: e.reciprocal(out=st[:, col_out:col_out + 1], in_=st[:, col_out:col_out + 1]), reads=[stB], writes=[stB])

    fw.op("pool", lambda e: e.memset(identf[:], 0.0), writes=[B_ident])
    fw.op("pool", lambda e: e.affine_select(out=identf[:], in_=identf[:], pattern=[[-1, 128]], compare_op=ALU.not_equal,
                                            fill=1.0, base=0, channel_multiplier=1), reads=[B_ident], writes=[B_ident])
    fw.op("pool", lambda e: e.tensor_copy(out=ident[:], in_=identf[:]), reads=[B_ident], writes=[B_ident])
    for i, gname in enumerate(["g_mix_pre", "g_ffn_pre", "g_ple"]):
        fw.dma("sp", gcol[:, i, :], W[gname].rearrange("o (c p) -> p (o c)", p=128), writes=[B_const], partial=True, slow=True)
    fw.dma("sp", cwT[:], W["conv_w"].rearrange("j (c p) -> p j c", p=128), writes=[B_const], partial=True, slow=True)
    fw.dma("sp", cbT[:], W["conv_b"].rearrange("o (c p) -> p (o c)", p=128), writes=[B_const], partial=True, slow=True)
    fw.dma("sp", bgT[:], W["b_gate"].rearrange("j (c p) -> p j c", p=128), writes=[B_const], partial=True, slow=True)
    fw.dma("sp", dcol[:], W["hyena_d"].rearrange("o (c p) -> p (o c)", p=128), writes=[B_const], partial=True, slow=True)
    fw.dma("sp", gsub[:], W["g_subln"].partition_broadcast(128), writes=[B_const], partial=True)
    for i, nm in enumerate(["lam_q1", "lam_k1", "lam_q2", "lam_k2"]):
        fw.dma("sp", lamt[:, i, :], W[nm].partition_broadcast(128), writes=[B_const], partial=True)
    fw.op("pool", lambda e: e.memset(cst[:, 0:1], -math.pi), writes=[B_const], partial=True)
    fw.op("pool", lambda e: e.memset(cst[:, 3:4], NORM_EPS), writes=[B_const], partial=True)
    tt("dve", lamt[:, 0, :], lamt[:, 0, :], lamt[:, 1, :], ALU.mult, [B_const], [B_const])
    tt("dve", lamt[:, 2, :], lamt[:, 2, :], lamt[:, 3, :], ALU.mult, [B_const], [B_const])
    fw.op("dve", lambda e: e.reduce_sum(out=lams[:, 0:1], in_=lamt[:, 0, :], axis=AX.X), reads=[B_const], writes=[B_const])
    fw.op("dve", lambda e: e.reduce_sum(out=lams[:, 1:2], in_=lamt[:, 2, :], axis=AX.X), reads=[B_const], writes=[B_const])
    act(lams[:, 0:2], lams[:, 0:2], AF.Exp, [B_const], [B_const])
    tt("dve", lams[:, 2:3], lams[:, 0:1], lams[:, 1:2], ALU.subtract, [B_const], [B_const])
    ts("dve", cst[:, 1:2], lams[:, 2:3], c.lam_init, None, ALU.add, None, [B_const], [B_const])
    ts("dve", cst[:, 2:3], cst[:, 1:2], -1.0, None, ALU.mult, None, [B_const], [B_const])
    ts("dve", gsub[:], gsub[:], 1.0 - c.lam_init, None, ALU.mult, None, [B_const], [B_const])

    def filter_phase(L, emb_d, dec_d, e0_d, Ft_d, kf_d, B_kf):
        fw.mark("filter_phase")
        N = 2 * L
        nt, nf = L // 128, N // 128
        half = nf // 2
        FE, FO = c.FE, c.FO
        TL = min(512, L)
        NTL = L // TL
        OFF = 7.0 * math.pi
        areset()
        h2, Bh2 = aalloc([L], F32, "h2")
        e0t, Be0 = aalloc([L], BF16, "e0t")
        ome0, Bome0 = aalloc([L], BF16, "ome0")
        w3t, Bw3 = aalloc([2 * HY], F32, "w3t")
        mark = ar["off"]
        w1t, Bw1 = aalloc([FO], F32, "w1t")
        w2t, Bw2 = aalloc([FO], F32, "w2t")
        fcol, Bfc = aalloc([8], F32, "fcol")
        embt, Bemb = aalloc([L], F32, "embt")
        h1, Bh1 = aalloc([L], F32, "h1")
        tmpp = apool(2, [TL], F32, "ftmp")
        fw.dma("sp", w1t[0:FE, :], W["filt_w1"], writes=[Bw1])
        fw.dma("sp", w2t[0:FO, :], W["filt_w2"], writes=[Bw2])
        fw.dma("sp", w3t[0:FO, :], W["filt_w3"], writes=[Bw3])
        fw.dma("sp", embt[0:FE, :], emb_d, writes=[Bemb])
        fw.dma("sp", fcol[0:FO, 0:1], W["filt_freq"].rearrange("o f -> f o"), writes=[Bfc], partial=True, slow=True)
        fw.dma("sp", fcol[0:FO, 1:2], W["filt_b1"].rearrange("o f -> f o"), writes=[Bfc], partial=True, slow=True)
        fw.dma("sp", fcol[0:FO, 2:3], W["filt_b2"].rearrange("o f -> f o"), writes=[Bfc], partial=True, slow=True)
        fw.dma("pool", e0t[:], e0_d.partition_broadcast(128), writes=[Be0])
        ts("dve", ome0[:], e0t[:], -1.0, 1.0, ALU.mult, ALU.add, [Be0], [Bome0])
        for k in (1, 2):
            ts("dve", fcol[0:FO, 2 + k:3 + k], fcol[0:FO, k:k + 1], fcol[0:FO, 0:1], None, ALU.mult, None, [Bfc], [Bfc])

        I32 = mybir.dt.int32
        kip = apool(2, [TL], I32, "fki")
        kfp = apool(2, [TL], F32, "fkf")
        PI_SAFE = 3.14159

        def sin_layer(wt, Bw, K, src, Bsrc, dst, Bdst, kcol):
            for tl in range(NTL):
                pb, pB = psum_all.next()
                sl = slice(tl * TL, (tl + 1) * TL)
                mm_group(pb[0:FO, 0:TL], [(wt[0:K, 0:FO], src[0:K, sl])], [Bw, Bsrc], [pB])
                tm, tB = tmpp.next()
                ts("dve", tm[0:FO, :], pb[0:FO, 0:TL], fcol[0:FO, 0:1], fcol[0:FO, 2 + kcol:3 + kcol], ALU.mult, ALU.add, [pB, Bfc], [tB])
                ki, kiB = kip.next()
                ts("dve", ki[0:FO, :], tm[0:FO, :], 1.0 / TWO_PI, None, ALU.mult, None, [tB], [kiB])
                kf, kfB = kfp.next()
                cp("dve", kf[0:FO, :], ki[0:FO, :], [kiB], [kfB])
                stt("dve", tm[0:FO, :], kf[0:FO, :], -TWO_PI, tm[0:FO, :], ALU.mult, ALU.add, [kfB, tB], [tB])
                ts("dve", tm[0:FO, :], tm[0:FO, :], -PI_SAFE, PI_SAFE, ALU.max, ALU.min, [tB], [tB])
                act(dst[0:FO, sl], tm[0:FO, :], AF.Sin, [tB], [Bdst], partial=True)

        sin_layer(w1t, Bw1, FE, embt, Bemb, h1, Bh1, 1)
        sin_layer(w2t, Bw2, FO, h1, Bh1, h2, Bh2, 2)

        for g in range(NCG):
            areset(mark)
            hf_tm, Bhf = aalloc([nt, 2, CG], BF16, "hf_tm")
            hw_, Bhw = [None, None], [None, None]
            hw_[0], Bhw[0] = aalloc([L], F32, "hfw")
            hw_[1], Bhw[1] = aalloc([L], F32, "hbw")
            hb16 = apool(2, [L], BF16, "hb16")
            decp = apool(2, [TL], F32, "decp")
            junk, Bjunk = aalloc([TL], BF16, "fjunk")
            asum, Bas = aalloc([2 * NTL + 4], F32, "asum")
            fpool = apool(2, [nt, 128], BF16, "fblk")
            outp = apool(3, [CG], F32, "fout")
            for j in range(NCH):
                ch = g * NCH + j
                for d_ in range(2):
                    col0 = d_ * HY + ch * 128
                    for tl in range(NTL):
                        sl = slice(tl * TL, (tl + 1) * TL)
                        pb, pB = psum_all.next()
                        mm_group(pb[:, 0:TL], [(w3t[0:FO, col0:col0 + 128], h2[0:FO, sl])], [Bw3, Bh2], [pB])
                        dt_, dB = decp.next()
                        fw.dma("sp", dt_[:], dec_d[ch * 128:(ch + 1) * 128, sl], writes=[dB])
                        tt("dve", hw_[d_][:, sl], pb[:, 0:TL], dt_[:], ALU.mult, [pB, dB], [Bhw[d_]], partial=(tl > 0))
                        act(junk[:], hw_[d_][:, sl], AF.Abs, [Bhw[d_]], [Bjunk, Bas], accum_out=asum[:, d_ * NTL + tl:d_ * NTL + tl + 1])
                na = 2 * NTL
                fw.op("dve", lambda e, na=na: e.reduce_sum(out=asum[:, na:na + 1], in_=asum[:, 0:na], axis=AX.X), reads=[Bas], writes=[Bas])
                ts("dve", asum[:, na:na + 1], asum[:, na:na + 1], NORM_EPS, None, ALU.add, None, [Bas], [Bas])
                fw.op("dve", lambda e, na=na: e.reciprocal(out=asum[:, na + 1:na + 2], in_=asum[:, na:na + 1]), reads=[Bas], writes=[Bas])
                rinv = asum[:, na + 1:na + 2]
                ts("dve", hw_[0][:], hw_[0][:], rinv, None, ALU.mult, None, [Bhw[0], Bas], [Bhw[0]])
                act(hw_[1][:], hw_[1][:], AF.Copy, [Bhw[1], Bas], [Bhw[1]], scale=rinv)
                for col in ([0] if L == Lq else [0, Lq]):
                    stt("dve", hw_[0][:, col:col + 1], e0t[:, col:col + 1], dcol[:, ch:ch + 1], hw_[0][:, col:col + 1], ALU.mult, ALU.add,
                        [Be0, B_const, Bhw[0]], [Bhw[0]])
                    tt("dve", hw_[1][:, col:col + 1], hw_[1][:, col:col + 1], ome0[:, col:col + 1], ALU.mult, [Bhw[1], Bome0], [Bhw[1]])
                hsb, BhsB = hb16.next()
                tt("pool", hsb[:], hw_[0][:], hw_[1][:], ALU.add, [Bhw[0], Bhw[1]], [BhsB])
                hdb, BhdB = hb16.next()
                tt("dve", hdb[:], hw_[0][:], hw_[1][:], ALU.subtract, [Bhw[0], Bhw[1]], [BhdB])
                for d_, (src, sB) in enumerate([(hsb, BhsB), (hdb, BhdB)]):
                    for tc0 in range(0, nt, 4):
                        n4 = min(4, nt - tc0)
                        pb, pB = psum_all.next()
                        pv = pb[:].bitcast(BF16)
                        transpose_group([pv[:, i * 128:(i + 1) * 128] for i in range(n4)],
                                        [src[:, (tc0 + i) * 128:(tc0 + i + 1) * 128] for i in range(n4)], [sB], [pB])
                        cp("act" if (tc0 // 4) % 2 else "dve", hf_tm[:, tc0:tc0 + n4, d_, j * 128:(j + 1) * 128],
                           pv[:, 0:n4 * 128].rearrange("p (a b) -> p a b", a=n4), [pB], [Bhf], partial=True)
            for fc in range(nf):
                fb, fB = fpool.next()
                fw.dma("sp", fb[:], Ft_d[fc], writes=[fB])
                pf, pfB = psum_all.next()
                sel = 0 if fc < half else 1
                mm_group(pf[:, 0:CG], [(fb[:, tc, :], hf_tm[:, tc, sel, :]) for tc in range(nt)], [fB, Bhf], [pfB])
                ot, oB = outp.next()
                cp("act" if fc % 2 else "dve", ot[:], pf[:, 0:CG], [pfB], [oB])
                if fc == half:
                    pn, pnB = psum_all.next()
                    mm_group(pn[0:1, 0:CG], [(fb[:, tc, 0:1], hf_tm[:, tc, 0, :]) for tc in range(nt)], [fB, Bhf], [pnB])
                    cp("dve", ot[0:1, :], pn[0:1, 0:CG], [pnB, oB], [oB])
                fw.dma(STQ, kf_d[fc // half, (fc % half) * 128:(fc % half + 1) * 128, g * CG:(g + 1) * CG], ot[:],
                       reads=[oB], writes=[B_kf], partial=True)

    def norm_phase(x_d, add_d, B_add, gpost_name, store_d, B_store, gidx, hT, BhT, col0, halo_d=None, p_d=None, pT=None, BpT=None, B_x=None):
        fw.mark("norm_phase")
        has_add = add_d is not None
        xin = apool(4, [D], F32, "xin")
        hbp = apool(2, [D], BF16, "hb")
        stp = apool(6, [8], F32, "nst")
        junk, Bjunk = aalloc([D], BF16, "njunk")
        if has_add:
            bin_ = apool(3, [D], F32, "bin")
            gpt, Bgp = aalloc([D], F32, "gpost")
            fw.dma("sp", gpt[:], W[gpost_name].partition_broadcast(128), writes=[Bgp])
        if p_d is not None:
            pin = apool(2, [PLE], F32, "pin")
            pbf = apool(2, [PLE], BF16, "pbf")
        nblk = NTB + (1 if halo_d is not None else 0)
        ctx = {}

        def ok(j):
            return 0 <= j < nblk

        def load(tb):
            is_halo = tb == NTB
            xt, xB = xin.next()
            st, sB = stp.next()
            d = ctx[tb] = dict(xt=xt, xB=xB, st=st, sB=sB, halo=is_halo)
            if is_halo:
                fw.op("pool", lambda e, xt=xt: e.memset(xt[:], 0.0), writes=[xB])
                fw.dma("sp", xt[0:2, :], halo_d, reads=[xB], writes=[xB], partial=True)
            else:
                fw.dma("sp", xt[:], x_d[tb * 128:(tb + 1) * 128, :], reads=([B_x] if B_x is not None else []), writes=[xB])
            if has_add:
                bt, bB = bin_.next()
                d.update(bt=bt, bB=bB)
                fw.dma("sp", bt[:], add_d[tb * 128:(tb + 1) * 128, :], reads=[B_add], writes=[bB])

        def sq(tb, which):
            d = ctx[tb]
            st, sB = d["st"], d["sB"]
            if which == "b":
                act(junk[:], d["bt"][:], AF.Square, [d["bB"]], [Bjunk], pwrites=[sB], accum_out=st[:, 0:1])
            else:
                act(junk[:], d["xt"][:], AF.Square, [d["xB"]], [Bjunk], pwrites=[sB], accum_out=st[:, 2:3])

        def r1(tb, c0_):
            d = ctx[tb]
            st, sB = d["st"], d["sB"]
            ts("dve", st[:, c0_ + 1:c0_ + 2], st[:, c0_:c0_ + 1], 1.0 / D, NORM_EPS, ALU.mult, ALU.add, [sB], [sB], partial=True)
            act(st[:, c0_ + 1:c0_ + 2], st[:, c0_ + 1:c0_ + 2], AF.Sqrt, [sB], [sB], partial=True)

        def r2(tb, c0_):
            d = ctx[tb]
            st, sB = d["st"], d["sB"]
            fw.op("dve", lambda e, st=st, c0_=c0_: e.reciprocal(out=st[:, c0_ + 1:c0_ + 2], in_=st[:, c0_ + 1:c0_ + 2]), reads=[sB], writes=[sB], partial=True)

        def combine(tb):
            d = ctx[tb]
            xt, xB, st, sB, bt, bB = d["xt"], d["xB"], d["st"], d["sB"], d["bt"], d["bB"]
            stt("dve", bt[:], bt[:], st[:, 1:2], gpt[:], ALU.mult, ALU.mult, [bB, sB, Bgp], [bB])
            tt("dve", xt[:], xt[:], bt[:], ALU.add, [xB, bB], [xB])
            if store_d is not None:
                fw.dma(STQ, store_d[tb * 128:(tb + 1) * 128, :], xt[:], reads=[xB], writes=[B_store], partial=True)

        def scale(tb):
            d = ctx[tb]
            hb, hB = hbp.next()
            d.update(hb=hb, hB=hB, pts=[])
            ts("dve", hb[:], d["xt"][:], d["st"][:, 3:4], None, ALU.mult, None, [d["xB"], d["sB"]], [hB])
            for c0 in range(0, KC, 4):
                n4 = min(4, KC - c0)
                pb, pB = psum_all.next()
                pv = pb[:].bitcast(BF16)
                transpose_group([pv[:, i * 128:(i + 1) * 128] for i in range(n4)],
                                [hb[:, (c0 + i) * 128:(c0 + i + 1) * 128] for i in range(n4)], [hB], [pB])
                d["pts"].append((c0, n4, pv, pB))

        def evac(tb):
            d = ctx.pop(tb)
            for c0, n4, pv, pB in d["pts"]:
                pv3 = pv[:, 0:n4 * 128].rearrange("p (a b) -> p a b", a=n4)
                gb = gcol[:, gidx, c0:c0 + n4]
                if d["halo"]:
                    for hc, dc_ in ((0, 0), (1, Lq + 1)):
                        tt("dve", hT[:, c0:c0 + n4, dc_:dc_ + 1], pv3[:, :, hc:hc + 1], gb.unsqueeze(2), ALU.mult,
                           [pB, B_const], [BhT], partial=True)
                else:
                    tt("dve", hT[:, c0:c0 + n4, col0 + tb * 128:col0 + (tb + 1) * 128], pv3,
                       gb.unsqueeze(2).to_broadcast([128, n4, 128]), ALU.mult, [pB, B_const], [BhT], partial=True)
            if p_d is not None and not d["halo"]:
                pt_, ptB = pin.next()
                fw.dma("sp", pt_[:], p_d[tb * 128:(tb + 1) * 128, :], writes=[ptB])
                pb16, pbB = pbf.next()
                cp("pool", pb16[:], pt_[:], [ptB], [pbB])
                pb, pB = psum_all.next()
                pv = pb[:].bitcast(BF16)
                transpose_group([pv[:, i * 128:(i + 1) * 128] for i in range(PC)],
                                [pb16[:, i * 128:(i + 1) * 128] for i in range(PC)], [pbB], [pB])
                cp("act", pT[:, :, tb * 128:(tb + 1) * 128], pv[:, 0:PC * 128].rearrange("p (a b) -> p a b", a=PC), [pB], [BpT], partial=True)

        for k in range(nblk + 3):
            if ok(k):
                load(k)
                sq(k, "b" if has_add else "x")
            if ok(k - 2):
                scale(k - 2)
            if has_add and ok(k - 1):
                combine(k - 1)
                sq(k - 1, "x")
            if ok(k - 2):
                evac(k - 2)
            if has_add:
                if ok(k):
                    r1(k, 0)
                if ok(k - 1):
                    r1(k - 1, 2)
                if ok(k):
                    r2(k, 0)
                if ok(k - 1):
                    r2(k - 1, 2)
            else:
                if ok(k):
                    r1(k, 2)
                    r2(k, 2)

    def inproj_hyena(hT, BhT, own, ctx_off):
        fw.mark("inproj_hyena")
        wsm = apool(6, [KC, 128], BF16, "wsm")
        zp = apool(2, [Lq + 2], F32, "z")
        zcp = apool(3, [Lq], F32, "zc")
        ustage = apool(2, [NTB, CG], BF16, "ustage")
        x0st = apool(2, [Lq], BF16, "x0st")
        ubf3 = apool(3, [Lq], BF16, "ubf3")
        pending = []

        def flush():
            while pending:
                ub, ubB, ust, uB, j = pending.pop(0)
                for tb0 in range(0, NTB, 4):
                    n4 = min(4, NTB - tb0)
                    pb, pB = psum_all.next()
                    pv = pb[:].bitcast(BF16)
                    transpose_group([pv[:, i * 128:(i + 1) * 128] for i in range(n4)],
                                    [ub[:, (tb0 + i) * 128:(tb0 + i + 1) * 128] for i in range(n4)], [ubB], [pB])
                    cp("act", ust[:, tb0:tb0 + n4, j * 128:(j + 1) * 128], pv[:, 0:n4 * 128].rearrange("p (a b) -> p a b", a=n4),
                       [pB], [uB], partial=True)

        whichs = [("x1", c.OX1), ("v", c.OV)] + ([("x0", c.OX0)] if own else [])

        def wl_h(key):
            ch_, off_ = key
            wb, wB = wsm.next()
            wload(wb[:], wB, W["w_in"], 0, KC, off_ + ch_ * 128, 128)
            return wb, wB
        wit = prefetched([(g_ * NCH + j_, off_) for g_ in range(NCG) for j_ in range(NCH) for _, off_ in whichs], wl_h, depth=3)
        for g in range(NCG):
            ust, uB = ustage.next()
            for j in range(NCH):
                ch = g * NCH + j
                res = {}
                for wi, (which, off) in enumerate(whichs):
                    gch = off // 128 + ch
                    _, (wb, wB) = next(wit)
                    z, zB = zp.next()
                    for tg in range(NTG):
                        pb, pB = psum_all.next()
                        mm_group(pb[:, 0:TT], [(wb[:, k, :], hT[:, k, 1 + tg * TT:1 + (tg + 1) * TT]) for k in range(KC)], [wB, BhT], [pB])
                        cp("act", z[:, 1 + tg * TT:1 + (tg + 1) * TT], pb[:, 0:TT], [pB], [zB], partial=True)
                    pb, pB = psum_all.next()
                    fw.tiny = True
                    mm_group(pb[:, 0:2], [(wb[:, k, :], hT[:, k, 0:Lq + 2:Lq + 1]) for k in range(KC)], [wB, BhT], [pB])
                    fw.tiny = False
                    cp("act", z[:, 0:Lq + 2:Lq + 1], pb[:, 0:2], [pB], [zB], partial=True)
                    if wi == 0:
                        flush()
                    zc, zcB = zcp.next()
                    ts("dve", zc[:], z[:, 1:Lq + 1], cwT[:, 1, gch:gch + 1], cbT[:, gch:gch + 1], ALU.mult, ALU.add, [zB, B_const], [zcB])
                    stt("dve", zc[:], z[:, 0:Lq], cwT[:, 0, gch:gch + 1], zc[:], ALU.mult, ALU.add, [zB, B_const, zcB], [zcB])
                    stt("dve", zc[:], z[:, 2:Lq + 2], cwT[:, 2, gch:gch + 1], zc[:], ALU.mult, ALU.add, [zB, B_const, zcB], [zcB])
                    res[which] = (zc, zcB)
                ub, ubB = ubf3.next()
                tt("dve", ub[:], res["x1"][0][:], res["v"][0][:], ALU.mult, [res["x1"][1], res["v"][1]], [ubB])
                pending.append((ub, ubB, ust, uB, j))
                if own:
                    xs, xsB = x0st.next()
                    cp("act", xs[:], res["x0"][0][:], [res["x0"][1]], [xsB])
                    fw.dma(STQ, x0T_s[ch * 128:(ch + 1) * 128, :], xs[:], reads=[xsB], writes=[B_x0T], partial=True)
            flush()
            fw.dma(STQ, u_s[g, ctx_off:ctx_off + Lq, :].rearrange("(tb p) c -> p tb c", p=128), ust[:], reads=[uB], writes=[B_u], partial=True)

    def inproj_qkv(hT, BhT, own, sp, ctx_off):
        fw.mark("inproj_qkv")
        wp = apool(3, [KC, QG], BF16, "wqkv")
        ropet, Brt = aalloc([NTB, 16], F32, "ropet")
        fw.dma("sp", ropet[:], rope[sp].rearrange("(tb p) k -> p tb k", p=128), writes=[Brt])
        NS = QG // 64
        tmpp = apool(2, [QG], F32, "qtmp")
        rtmp = apool(2, [4, NS, 8], F32, "rtmp")
        qkb = apool(3, [QG], BF16, "qkb")
        stage = apool(2, [QG // 128, Lq], BF16, "qkstage")
        wpv = apool(NQG, [KC, QG], BF16, "wv")
        vstate = {}

        def emit_v_loads():
            vst = apool(3, [NH, 129], BF16, "vst")
            for vt_, vB in vst.items:
                fw.op("pool", lambda e, vt_=vt_: e.memset(vt_[:], 1.0), writes=[vB])
            wbs = []
            for cg in range(NQG):
                wb, wB = wpv.next()
                wload(wb[:], wB, W["w_in"], 0, KC, c.OVA + cg * QG, QG)
                wbs.append((wb, wB))
            vstate['vst'] = vst
            vstate['wbs'] = wbs

        def wl_qk(key):
            off_, cg_ = key
            wb, wB = wp.next()
            wload(wb[:], wB, W["w_in"], 0, KC, off_ + cg_ * QG, QG)
            return wb, wB
        qk_list = [(which_, off_, cg_) for which_, off_ in ([("q", c.OQ)] if own else []) + [("k", c.OK)] for cg_ in range(NQG)]
        wit = prefetched([(off_, cg_) for _, off_, cg_ in qk_list], wl_qk, depth=1)
        for which, off, cg in qk_list:
            if True:
                _, (wb, wB) = next(wit)
                if not vstate:
                    emit_v_loads()
                stg, sB = stage.next()
                n4 = QG // 128

                def front(tb, wb=wb, wB=wB):
                    pb, pB = psum_all.next()
                    mm_group(pb[:, 0:QG], [(hT[:, k, 1 + tb * 128:1 + (tb + 1) * 128], wb[:, k, :]) for k in range(KC)], [BhT, wB], [pB])
                    tm, tB = tmpp.next()
                    cp("act", tm[:], pb[:, 0:QG], [pB], [tB])
                    v3 = tm[:].rearrange("p (s d) -> p s d", d=64)
                    x1, x2 = v3[:, :, 0:8], v3[:, :, 8:16]
                    cos = ropet[:, tb, 0:8].unsqueeze(1).to_broadcast([128, NS, 8])
                    sin = ropet[:, tb, 8:16].unsqueeze(1).to_broadcast([128, NS, 8])
                    r, rB = rtmp.next()
                    tt("dve", r[:, 0], x1, cos, ALU.mult, [tB, Brt], [rB])
                    tt("dve", r[:, 1], x2, sin, ALU.mult, [tB, Brt], [rB])
                    tt("dve", r[:, 2], x2, cos, ALU.mult, [tB, Brt], [rB])
                    tt("dve", r[:, 3], x1, sin, ALU.mult, [tB, Brt], [rB])
                    tt("dve", x1, r[:, 0], r[:, 1], ALU.subtract, [rB, tB], [tB])
                    tt("dve", x2, r[:, 2], r[:, 3], ALU.add, [rB, tB], [tB])
                    qb, qB = qkb.next()
                    cp("dve", qb[:], tm[:], [tB], [qB])
                    return qb, qB

                def back(tb, qb, qB, stg=stg, sB=sB):
                    pt, ptB = psum_all.next()
                    pv = pt[:].bitcast(BF16)
                    transpose_group([pv[:, i * 128:(i + 1) * 128] for i in range(n4)], [qb[:, i * 128:(i + 1) * 128] for i in range(n4)], [qB], [ptB])
                    cp("act", stg[:, :, tb * 128:(tb + 1) * 128], pv[:, 0:n4 * 128].rearrange("p (a b) -> p a b", a=n4), [ptB], [sB], partial=True)

                prev = None
                for tb in range(NTB + 1):
                    cur = front(tb) if tb < NTB else None
                    if prev is not None:
                        back(tb - 1, *prev)
                    prev = cur
                if which == "q":
                    fw.dma(STQ, qT_s[cg * QG:(cg + 1) * QG, :].rearrange("(a p) t -> p a t", p=128), stg[:], reads=[sB], writes=[B_qT], partial=True)
                else:
                    fw.dma(STQ, kT_s[cg * QG:(cg + 1) * QG, ctx_off:ctx_off + Lq].rearrange("(a p) t -> p a t", p=128), stg[:],
                           reads=[sB], writes=[B_kT], partial=True)
        vst, wbs = vstate['vst'], vstate['wbs']
        for tb in range(NTB):
            vt_, vB = vst.next()
            for cg in range(NQG):
                wb, wB = wbs[cg]
                pb, pB = psum_all.next()
                mm_group(pb[:, 0:QG], [(hT[:, k, 1 + tb * 128:1 + (tb + 1) * 128], wb[:, k, :]) for k in range(KC)], [BhT, wB], [pB])
                nh = QG // 128
                cp("act", vt_[:, cg * nh:(cg + 1) * nh, 0:128], pb[:, 0:QG].rearrange("p (a b) -> p a b", a=nh), [pB], [vB], partial=True)
            fw.dma(STQ, v_s[ctx_off + tb * 128:ctx_off + (tb + 1) * 128, :, :], vt_[:], reads=[vB], writes=[B_v], partial=True)

    def inproj_gates(hT, BhT):
        fw.mark("inproj_gates")
        wg = apool(3, [KC, DG], BF16, "wgate")
        gst = apool(3, [Lq], F32, "gst")
        def wl_g(key):
            gi_, cg_ = key
            wb, wB = wg.next()
            wload(wb[:], wB, W["w_in"], 0, KC, c.OG0 + gi_ * D + cg_ * DG, DG)
            return wb, wB
        for (gi, cg), (wb, wB) in prefetched([(gi_, cg_) for gi_ in range(2) for cg_ in range(NDG)], wl_g, depth=1):
            if True:
                for j in range(DG // 128):
                    dc = cg * (DG // 128) + j
                    gt, gB = gst.next()
                    for tg in range(NTG):
                        pb, pB = psum_all.next()
                        mm_group(pb[:, 0:TT], [(wb[:, k, j * 128:(j + 1) * 128], hT[:, k, 1 + tg * TT:1 + (tg + 1) * TT]) for k in range(KC)], [wB, BhT], [pB])
                        act(gt[:, tg * TT:(tg + 1) * TT], pb[:, 0:TT], AF.Sigmoid, [pB, B_const], [gB], partial=True, bias=bgT[:, gi, dc:dc + 1])
                    fw.dma(STQ, gates_s[gi, dc * 128:(dc + 1) * 128, :], gt[:], reads=[gB], writes=[B_gates], partial=True)

    def attn_phase(Lc):
        fw.mark("attn_phase")
        areset()
        nkb = Lc // 128
        NQS = TT // 128
        kTt = apool(2, [Lc], BF16, "kTt")
        qzp = apool(2, [2, Lq], BF16, "qz")
        vtp = apool(2, [nkb, 129], BF16, "vtp")
        pTp = apool(4, [TT], BF16, "pT")
        osb = apool(2, [NQS, 128], F32, "osb")
        ob16 = apool(2, [NQS, 128], BF16, "ob16")
        stp = apool(3, [24], F32, "ast")
        asbp = apool(2, [4, 2, 129], F32, "asb")
        junk, Bjunk = aalloc([NQS, 128], BF16, "ajunk")
        ystage = apool(2, [Lq], BF16, "ystage")
        scores = Rot(list(zip(banks[4:7], bankB[4:7])))
        trb, trB = banks[7], bankB[7]
        for qz, qB in qzp.items:
            fw.op("pool", lambda e, qz=qz: e.memset(qz[:], 0.0), writes=[qB])

        def acc(i, qs):
            b = i * 2 + qs // 2
            o = (qs % 2) * 256
            return banks[b][:, o:o + 129], bankB[b]

        def load_head(h):
            kt, kB = kTt.next()
            fw.dma("sp", kt[:], kT_s[h * 128:(h + 1) * 128, 0:Lc], reads=[B_kT], writes=[kB])
            qz, qB = qzp.next()
            fw.dma("sp", qz[0:64, 0, :], qT_s[h * 128:h * 128 + 64, :], reads=[B_qT, qB], writes=[qB], partial=True)
            fw.dma("sp", qz[64:128, 1, :], qT_s[h * 128 + 64:(h + 1) * 128, :], reads=[B_qT, qB], writes=[qB], partial=True)
            vv, vB = vtp.next()
            fw.dma("sp", vv[:], v_s[0:Lc, h, :].rearrange("(kb p) e -> p kb e", p=128), reads=[B_v], writes=[vB])
            return kt, kB, qz, qB, vv, vB

        pending = []
        pending_a = []

        def flush_a():
            while pending_a:
                pending_a.pop(0)()

        def flush():
            flush_a()
            while pending:
                pending.pop(0)()

        nxt = load_head(0)
        for h in range(NH):
            kt, kB, qz, qB, vv, vB = nxt
            if h + 1 < NH:
                nxt = load_head(h + 1)
            ys, ysB = ystage.next()
            for qt in range(NTG):
                steps = [(kb, i) for kb in range(nkb) for i in range(2)]
                n = len(steps)

                def qk(s_):
                    kb, i = steps[s_]
                    sc, scB = scores.next()
                    mm_group(sc[:, 0:TT], [(kt[:, kb * 128:(kb + 1) * 128], qz[:, i, qt * TT:(qt + 1) * TT])], [kB, qB], [scB])
                    return sc, scB

                scq = [qk(0), qk(1)]
                for s_ in range(n):
                    kb, i = steps[s_]
                    sc, scB = scq.pop(0)
                    pT, pTB = pTp.next()
                    act(pT[:], sc[:, 0:TT], AF.Exp, [scB], [pTB], scale=0.125)
                    if s_ + 2 < n:
                        scq.append(qk(s_ + 2))
                    if s_ == min(6, n - 1):
                        flush_a()
                    if s_ == min(14, n - 1):
                        flush()
                    for qs in range(NQS):
                        a, aB = acc(i, qs)
                        fw.op("pe", lambda e, a=a, pT=pT, qs=qs, vv=vv, kb=kb: e.matmul(
                            a, lhsT=pT[:, qs * 128:(qs + 1) * 128], rhs=vv[:, kb, :], start=(kb == 0 and qs % 2 == 0),
                            stop=(kb == nkb - 1), skip_group_check=True),
                            reads=[pTB, vB], writes=[aB], inc=(qs == NQS - 1))
                s4, sB = stp.next()
                o, oB = osb.next()
                asb, asB = asbp.next()
                for b in sorted({i_ * 2 + qs_ // 2 for i_ in range(2) for qs_ in range(NQS)}):
                    cp("act", asb[:, b, :, :], banks[b][:, :].rearrange("p (t c) -> p t c", t=2)[:, :, 0:129], [bankB[b]], [asB], partial=True)

                def acc_sb(i_, qs_):
                    return asb[:, i_ * 2 + qs_ // 2, qs_ % 2, :], asB
                for qs in range(NQS):
                    a0, B0 = acc_sb(0, qs)
                    a1, B1 = acc_sb(1, qs)
                    fw.op("dve", lambda e, s4=s4, a0=a0, qs=qs: e.reciprocal(out=s4[:, qs:qs + 1], in_=a0[:, 128:129]), reads=[B0], writes=[sB], partial=True)
                    fw.op("dve", lambda e, s4=s4, a1=a1, qs=qs: e.reciprocal(out=s4[:, 4 + qs:5 + qs], in_=a1[:, 128:129]), reads=[B1], writes=[sB], partial=True)
                ts("dve", s4[:, 8:8 + NQS], s4[:, 4:4 + NQS], cst[:, 2:3], None, ALU.mult, None, [sB, B_const], [sB])
                for qs in range(NQS):
                    a0, B0 = acc_sb(0, qs)
                    a1, B1 = acc_sb(1, qs)
                    ts("dve", o[:, qs, :], a0[:, 0:128], s4[:, qs:qs + 1], None, ALU.mult, None, [B0, sB], [oB], partial=(qs > 0))
                    stt("dve", o[:, qs, :], a1[:, 0:128], s4[:, 8 + qs:9 + qs], o[:, qs, :], ALU.mult, ALU.add, [B1, sB, oB], [oB], partial=True)
                def mid(o=o, oB=oB, s4=s4, sB=sB):
                    for qs in range(NQS):
                        act(junk[:, qs, :], o[:, qs, :], AF.Square, [oB], [Bjunk, sB], partial=(qs > 0), accum_out=s4[:, 12 + qs:13 + qs])
                    act(s4[:, 16:16 + NQS], s4[:, 12:12 + NQS], AF.Ln, [sB, B_const], [sB], scale=1.0 / 128, bias=cst[:, 3:4])
                    act(s4[:, 16:16 + NQS], s4[:, 16:16 + NQS], AF.Exp, [sB], [sB], scale=-0.5)
                    ob, obB = ob16.next()
                    for qs in range(NQS):
                        stt("dve", ob[:, qs, :], o[:, qs, :], s4[:, 16 + qs:17 + qs], gsub[:], ALU.mult, ALU.mult, [oB, sB, B_const], [obB], partial=(qs > 0))
                    return ob, obB

                def back(mid=mid, ys=ys, ysB=ysB, qt=qt):
                    ob, obB = mid.result
                    pv = trb[:].bitcast(BF16)
                    transpose_group([pv[:, qs * 128:(qs + 1) * 128] for qs in range(NQS)], [ob[:, qs, :] for qs in range(NQS)], [obB], [trB])
                    cp("act", ys[:, qt * TT:(qt + 1) * TT], pv[:, 0:TT], [trB], [ysB], partial=True)

                def run_mid(mid=mid):
                    mid.result = mid()
                pending_a.append(run_mid)
                pending.append(back)
            pending.append(lambda h=h, ys=ys, ysB=ysB: fw.dma(STQ, yattT_s[h * 128:(h + 1) * 128, :], ys[:], reads=[ysB], writes=[B_yattT], partial=True))
        flush_a()
        flush()

    def hyena_phase(Lc, Ft_d, Gt_d, kf_d, B_kf):
        fw.mark("hyena_phase")
        N = 2 * Lc
        nt, nf = Lc // 128, N // 128
        half = nf // 2
        for g in range(NCG):
            areset()
            Yf, BYf = aalloc([nf, CG], BF16, "Yf")
            mark = ar["off"]
            u_sb, Bu = aalloc([nt, CG], BF16, "u_sb")
            fw.dma("sp", u_sb[:], u_s[g, 0:Lc, :].rearrange("(tb p) c -> p tb c", p=128), reads=[B_u], writes=[Bu])
            fpool = apool(4, [nt, 128], BF16, "hfblk")
            kp = apool(4, [CG], F32, "hkf")
            tp = apool(8, [CG], F32, "htmp")
            for jp in range(half):
                fre, fB1 = fpool.next()
                fw.dma("sp", fre[:], Ft_d[jp], writes=[fB1])
                fim, fB2 = fpool.next()
                fw.dma("sp", fim[:], Ft_d[half + jp], writes=[fB2])
                pre, pB1 = psum_all.next()
                mm_group(pre[:, 0:CG], [(fre[:, tc, :], u_sb[:, tc, :]) for tc in range(nt)], [fB1, Bu], [pB1])
                pim, pB2 = psum_all.next()
                mm_group(pim[:, 0:CG], [(fim[:, tc, :], u_sb[:, tc, :]) for tc in range(nt)], [fB2, Bu], [pB2])
                kre, kB1 = kp.next()
                fw.dma("sp", kre[:], kf_d[0, jp * 128:(jp + 1) * 128, g * CG:(g + 1) * CG], reads=[B_kf], writes=[kB1])
                kim, kB2 = kp.next()
                fw.dma("sp", kim[:], kf_d[1, jp * 128:(jp + 1) * 128, g * CG:(g + 1) * CG], reads=[B_kf], writes=[kB2])
                (t1, tB1), (t2, tB2), (t3, tB3), (t4, tB4) = tp.next(), tp.next(), tp.next(), tp.next()
                tt("dve", t1[:], pre[:, 0:CG], kre[:], ALU.mult, [pB1, kB1], [tB1])
                tt("dve", t2[:], pim[:, 0:CG], kim[:], ALU.mult, [pB2, kB2], [tB2])
                tt("dve", t3[:], pre[:, 0:CG], kim[:], ALU.mult, [pB1, kB2], [tB3])
                tt("dve", t4[:], pim[:, 0:CG], kre[:], ALU.mult, [pB2, kB1], [tB4])
                tt("pool", Yf[:, jp, :], t1[:], t2[:], ALU.subtract, [tB1, tB2], [BYf], partial=True)
                tt("pool", Yf[:, half + jp, :], t3[:], t4[:], ALU.add, [tB3, tB4], [BYf], partial=True)
                if jp == 0:
                    cp("pool", Yf[0:1, 0, :], t1[0:1, :], [tB1, BYf], [BYf], partial=True)
                    cp("pool", Yf[0:1, half, :], t2[0:1, :], [tB2, BYf], [BYf], partial=True)
            if Lc > Lq:
                areset(mark)
            gpool = apool(2, [nf, 128], BF16, "hgblk")
            ybf = apool(2, [CG], BF16, "hybf")
            x0sb, Bx0 = aalloc([NCH, Lq], BF16, "x0sb")
            ystg, Bys = aalloc([NCH, Lq], BF16, "hystg")
            fw.dma("sp", x0sb[:], x0T_s[g * CG:(g + 1) * CG, :].rearrange("(j p) t -> p j t", p=128), reads=[B_x0T], writes=[Bx0])
            for tb in range(NTB):
                gb_, gB = gpool.next()
                fw.dma("sp", gb_[:], Gt_d[tb], writes=[gB])
                py, pyB = psum_all.next()
                mm_group(py[:, 0:CG], [(gb_[:, fc, :], Yf[:, fc, :]) for fc in range(nf)], [gB, BYf], [pyB])
                yb, ybB = ybf.next()
                cp("act", yb[:], py[:, 0:CG], [pyB], [ybB])
                ptr, ptB = psum_all.next()
                pv = ptr[:].bitcast(BF16)
                transpose_group([pv[:, j * 128:(j + 1) * 128] for j in range(NCH)], [yb[:, j * 128:(j + 1) * 128] for j in range(NCH)], [ybB], [ptB])
                tt("dve", ystg[:, :, tb * 128:(tb + 1) * 128], pv[:, 0:NCH * 128].rearrange("p (a b) -> p a b", a=NCH),
                   x0sb[:, :, tb * 128:(tb + 1) * 128], ALU.mult, [ptB, Bx0], [Bys], partial=True)
            fw.dma(STQ, yhyT_s[g * CG:(g + 1) * CG, :].rearrange("(j p) t -> p j t", p=128), ystg[:], reads=[Bys], writes=[B_yhyT], partial=True)
    def mix_phase():
        fw.mark("mix_phase")
        areset()
        yh, _ = aalloc([HC, Lq], BF16, "yh")
        ya, _ = aalloc([AC, Lq], BF16, "ya")
        Byhs = [fw.newbuf("yhtg") for _ in range(NTG)]
        Byas = [fw.newbuf("yatg") for _ in range(NTG)]
        for tg in range(NTG):
            tsl = slice(tg * TT, (tg + 1) * TT)
            fw.dma("sp", yh[:, :, tsl], yhyT_s[:, tsl].rearrange("(c p) t -> p c t", p=128), reads=[B_yhyT], writes=[Byhs[tg]])
            fw.dma("sp", ya[:, :, tsl], yattT_s[:, tsl].rearrange("(c p) t -> p c t", p=128), reads=[B_yattT], writes=[Byas[tg]])
        wh = apool(2, [HC, DG], BF16, "wh")
        wa = apool(2, [AC, DG], BF16, "wa")
        gp = apool(4, [Lq], F32, "gp")
        t1p = apool(2, [TT], F32, "mt1")
        t2p = apool(2, [TT], F32, "mt2")
        mst = apool(2, [Lq], BF16, "mst")
        def wl_m(cg_):
            whb, whB = wh.next()
            wload(whb[:], whB, W["w_hy_out"], 0, HC, cg_ * DG, DG)
            wab, waB = wa.next()
            wload(wab[:], waB, W["w_att_out"], 0, AC, cg_ * DG, DG)
            return whb, whB, wab, waB
        for cg, (whb, whB, wab, waB) in prefetched(range(NDG), wl_m, depth=1):
            for j in range(DG // 128):
                dc = cg * (DG // 128) + j
                g0, gB0 = gp.next()
                fw.dma("sp", g0[:], gates_s[0, dc * 128:(dc + 1) * 128, :], reads=[B_gates], writes=[gB0])
                g1, gB1 = gp.next()
                fw.dma("sp", g1[:], gates_s[1, dc * 128:(dc + 1) * 128, :], reads=[B_gates], writes=[gB1])
                ms, msB = mst.next()
                for tg in range(NTG):
                    sl = slice(tg * TT, (tg + 1) * TT)
                    pa, paB = psum_all.next()
                    mm_group(pa[:, 0:TT], [(whb[:, k, j * 128:(j + 1) * 128], yh[:, k, sl]) for k in range(HC)], [whB, Byhs[tg]], [paB])
                    pb, pbB = psum_all.next()
                    mm_group(pb[:, 0:TT], [(wab[:, k, j * 128:(j + 1) * 128], ya[:, k, sl]) for k in range(AC)], [waB, Byas[tg]], [pbB])
                    t1, tB1 = t1p.next()
                    tt("dve", t1[:], pa[:, 0:TT], g0[:, sl], ALU.mult, [paB, gB0], [tB1])
                    t2, tB2 = t2p.next()
                    tt("dve", t2[:], pb[:, 0:TT], g1[:, sl], ALU.mult, [pbB, gB1], [tB2])
                    tt("dve", ms[:, sl], t1[:], t2[:], ALU.add, [tB1, tB2], [msB], partial=True)
                fw.dma(STQ, mT_s[dc * 128:(dc + 1) * 128, :], ms[:], reads=[msB], writes=[B_mT], partial=True)

    def outproj_phase():
        fw.mark("outproj_phase")
        areset()
        mT, _ = aalloc([KC, Lq], BF16, "mT")
        BmTs = [fw.newbuf("mTtg") for _ in range(NTG)]
        for tg in range(NTG):
            fw.dma("sp", mT[:, :, tg * TT:(tg + 1) * TT], mT_s[:, tg * TT:(tg + 1) * TT].rearrange("(c p) t -> p c t", p=128), reads=[B_mT], writes=[BmTs[tg]])
        wp = apool(3, [KC, DG], BF16, "wout")
        stg = apool(3, [DG], F32, "ostg")
        def wl_o(cg_):
            wb, wB = wp.next()
            wload(wb[:], wB, W["w_out"], 0, KC, cg_ * DG, DG)
            return wb, wB
        for cg, (wb, wB) in prefetched(range(NDG), wl_o, depth=2):
            for tb in range(NTB):
                pb, pB = psum_all.next()
                mm_group(pb[:, 0:DG], [(mT[:, k, tb * 128:(tb + 1) * 128], wb[:, k, :]) for k in range(KC)], [BmTs[tb * 128 // TT], wB], [pB])
                st, sB = stg.next()
                cp("act" if tb % 2 else "dve", st[:], pb[:, 0:DG], [pB], [sB])
                fw.dma(STQ, br_s[tb * 128:(tb + 1) * 128, cg * DG:(cg + 1) * DG], st[:], reads=[sB], writes=[B_br], partial=True)

    FG = 512 if DFF % 512 == 0 else (256 if DFF % 256 == 0 else 128)
    NFG = DFF // FG
    FP = next(p for p in (11, 16, 8, 6, 4, 3, 2, 1) if FC % p == 0)
    NFP = FC // FP

    def ffn_up_phase(h2T, Bh2):
        fw.mark("ffn_up_phase")
        wpg = apool(2, [KC, FG], BF16, "wfg")
        wpu = apool(2, [KC, FG], BF16, "wfu")
        sil = apool(2, [TT], F32, "sil")
        ast = apool(2, [Lq], BF16, "ast")
        def wl_f(fg_):
            wg_, wgB = wpg.next()
            wload(wg_[:], wgB, W["w_ffn_in"], 0, KC, fg_ * FG, FG)
            wu_, wuB = wpu.next()
            wload(wu_[:], wuB, W["w_ffn_in"], 0, KC, DFF + fg_ * FG, FG)
            return wg_, wgB, wu_, wuB
        for fg, (wg_, wgB, wu_, wuB) in prefetched(range(NFG), wl_f, depth=1):
            for j in range(FG // 128):
                fc = fg * (FG // 128) + j
                as_, aB = ast.next()
                for tg in range(NTG):
                    sl = slice(tg * TT, (tg + 1) * TT)
                    pg, pgB = psum_all.next()
                    mm_group(pg[:, 0:TT], [(wg_[:, k, j * 128:(j + 1) * 128], h2T[:, k, sl]) for k in range(KC)], [wgB, Bh2], [pgB])
                    pu, puB = psum_all.next()
                    mm_group(pu[:, 0:TT], [(wu_[:, k, j * 128:(j + 1) * 128], h2T[:, k, sl]) for k in range(KC)], [wuB, Bh2], [puB])
                    s, sB = sil.next()
                    act(s[:], pg[:, 0:TT], AF.Silu, [pgB], [sB])
                    tt("dve", as_[:, sl], pu[:, 0:TT], s[:], ALU.mult, [puB, sB], [aB], partial=True)
                fw.dma(STQ, aT_s[fc * 128:(fc + 1) * 128, :], as_[:], reads=[aB], writes=[B_aT], partial=True)

    def ffn_down_phase():
        fw.mark("ffn_down_phase")
        areset()
        atp = apool(2, [FC, TT], BF16, "aTt")
        wp = apool(4, [FP, DG], BF16, "wdown")
        stg = apool(3, [DG], F32, "dstg")
        NQS = TT // 128
        def wl_d(key):
            _, cg_, pc_ = key
            wb, wB = wp.next()
            src = wfo16[pc_ * FP * 128:(pc_ + 1) * FP * 128, cg_ * DG:(cg_ + 1) * DG].rearrange("(c p) n -> p c n", p=128)
            fw.dma("sp", wb[:], src, reads=[B_wfo], writes=[wB])
            return wb, wB
        wit = prefetched([(tg_, cg_, pc_) for tg_ in range(NTG) for cg_ in range(NDG) for pc_ in range(NFP)], wl_d, depth=2)

        atB = {}

        def al_d(tg_):
            at, aB0 = atp.next()
            key = id(aB0)
            if key not in atB:
                atB[key] = [aB0] + [fw.newbuf("aTpc") for _ in range(NFP - 1)]
            bl = atB[key]
            for pc_ in range(NFP):
                fw.dma("sp", at[:, pc_ * FP:(pc_ + 1) * FP, :],
                       aT_s[pc_ * FP * 128:(pc_ + 1) * FP * 128, tg_ * TT:(tg_ + 1) * TT].rearrange("(k p) t -> p k t", p=128),
                       reads=[B_aT], writes=[bl[pc_]])
            return at, bl
        for tg, (at, aB) in prefetched(range(NTG), al_d, depth=1):
            for cg in range(NDG):
                accs = [psum_all.next() for _ in range(NQS)]
                for pc in range(NFP):
                    _, (wb, wB) = next(wit)
                    for qs in range(NQS):
                        pa, paB = accs[qs]
                        for kk in range(FP):
                            k = pc * FP + kk
                            fw.op("pe", lambda e, pa=pa, at=at, k=k, qs=qs, wb=wb, kk=kk: e.matmul(
                                pa[:, 0:DG], lhsT=at[:, k, qs * 128:(qs + 1) * 128], rhs=wb[:, kk, :], start=(k == 0), stop=(k == FC - 1)),
                                reads=[aB[pc], wB], writes=[paB], inc=(kk == FP - 1))
                for qs in range(NQS):
                    pa, paB = accs[qs]
                    st, sB = stg.next()
                    cp("act" if qs % 2 else "dve", st[:], pa[:, 0:DG], [paB], [sB])
                    r0 = tg * TT + qs * 128
                    fw.dma(STQ, br_s[r0:r0 + 128, cg * DG:(cg + 1) * DG], st[:], reads=[sB], writes=[B_br], partial=True)

    def ple_phase(seg, h3T, Bh3, pT, BpT):
        fw.mark("ple_phase")
        wpg = apool(2, [KC, DG], BF16, "wpleg")
        wpi = apool(2, [PC, DG], BF16, "wplei")
        x2t = apool(3, [DG], F32, "x2t")
        et = apool(2, [DG], F32, "et")
        ot = apool(3, [DG], F32, "ot")
        def wl_p(cg_):
            wg_, wgB = wpg.next()
            wload(wg_[:], wgB, W["w_ple_gate"], 0, KC, cg_ * DG, DG)
            wi_, wiB = wpi.next()
            wload(wi_[:], wiB, W["w_ple_in"], 0, PC, cg_ * DG, DG)
            return wg_, wgB, wi_, wiB
        for cg, (wg_, wgB, wi_, wiB) in prefetched(range(NDG), wl_p, depth=1):
            for tb in range(NTB):
                tsl = slice(tb * 128, (tb + 1) * 128)
                pe_, peB = psum_all.next()
                mm_group(pe_[:, 0:DG], [(h3T[:, k, tsl], wg_[:, k, :]) for k in range(KC)], [Bh3, wgB], [peB])
                pp, ppB = psum_all.next()
                mm_group(pp[:, 0:DG], [(pT[:, k, tsl], wi_[:, k, :]) for k in range(PC)], [BpT, wiB], [ppB])
                e_, eB = et.next()
                act(e_[:], pe_[:, 0:DG], AF.Sigmoid, [peB], [eB])
                x2, xB = x2t.next()
                fw.dma("sp", x2[:], x2_s[tsl, cg * DG:(cg + 1) * DG], reads=[B_x2], writes=[xB])
                o, oB = ot.next()
                tt("dve", o[:], pp[:, 0:DG], e_[:], ALU.mult, [ppB, eB], [oB])
                tt("dve", o[:], o[:], x2[:], ALU.add, [oB, xB], [oB])
                fw.dma(STQ, yq[seg, tsl, cg * DG:(cg + 1) * DG], o[:], reads=[oB], writes=[B_out], partial=True)

    for r0 in range(0, DFF, 512):
        r1 = min(DFF, r0 + 512)
        fw.dma("pool", wfo16[r0:r1, :], W["w_ffn_out"][r0:r1, :], writes=[B_wfo], partial=True)
    filter_phase(LP, embP, decP, e0P, FtP, kfP, B_kfP)
    filter_phase(LS, embS, decS, e0S, FtS, kfS, B_kfS)
    for seg in range(3):
        subs = [(seg, xq[seg], True, 0)]
        if seg == 2:
            subs.append((3, xo, False, Lq))
        for sp, x_d, own, ctx_off in subs:
            areset()
            hT, BhT = aalloc([KC, Lq + 2], BF16, "hT")
            mark = ar["off"]
            norm_phase(x_d, None, None, None, None, None, 0, hT, BhT, 1, halo_d=xhalo[sp])
            areset(mark)
            inproj_hyena(hT, BhT, own, ctx_off)
            areset(mark)
            inproj_qkv(hT, BhT, own, sp, ctx_off)
            if own:
                areset(mark)
                inproj_gates(hT, BhT)
        Lc = 2 * Lq if seg == 2 else Lq
        if seg == 2:
            hyena_phase(Lc, FtS, GtS, kfS, B_kfS)
        else:
            hyena_phase(Lc, FtP, GtP, kfP, B_kfP)
        attn_phase(Lc)
        mix_phase()
        outproj_phase()
        areset()
        h2T, Bh2 = aalloc([KC, Lq], BF16, "h2T")
        mark = ar["off"]
        norm_phase(xq[seg], br_s, B_br, "g_mix_post", x1_s, B_x1, 1, h2T, Bh2, 0)
        areset(mark)
        ffn_up_phase(h2T, Bh2)
        ffn_down_phase()
        areset()
        h3T, Bh3 = aalloc([KC, Lq], BF16, "h3T")
        pT, BpT = aalloc([PC, Lq], BF16, "pT")
        mark = ar["off"]
        norm_phase(x1_s, br_s, B_br, "g_ffn_post", x2_s, B_x2, 2, h3T, Bh3, 0, p_d=pq[seg], pT=pT, BpT=BpT, B_x=B_x1)
        areset(mark)
        ple_phase(seg, h3T, Bh3, pT, BpT)
    fw.finish()
    fw.mark("end")
    nc._fw_stats = fw.stats
    nc._fw_marks = fw.marks
    return nc


def make_in_maps(cfg, inputs):
    c = cfg
    Lq, nco = c.Lq, c.ncores
    f32 = np.float32
    xp = np.asarray(inputs["x_prompt"], f32)
    xs = np.asarray(inputs["x_sample"], f32)
    pp = np.asarray(inputs["p_prompt"], f32)[0]
    ps = np.asarray(inputs["p_sample"], f32)[0]
    wmap = {}
    for k, v in inputs.items():
        if k in ("x_prompt", "x_sample", "p_prompt", "p_sample"):
            continue
        a = np.asarray(v, f32)[0]
        if a.ndim == 1:
            a = a[None, :]
        wmap[k] = np.ascontiguousarray(a)
    ordP = np.arange(c.LP)
    FtP, GtP = dft_tables(ordP, ordP)
    embP, decP, e0P = filter_tables(c.LP, ordP, c.HY)
    per_half = {}
    for hf in range(2):
        own = np.arange(hf * Lq, (hf + 1) * Lq)
        oth = np.arange((1 - hf) * Lq, (2 - hf) * Lq)
        order = np.concatenate([own, oth])
        FtS, GtS = dft_tables(order, own)
        embS, decS, e0S = filter_tables(c.LS, order, c.HY)
        per_half[hf] = (FtS, GtS, embS, decS, e0S, own, oth)
    in_maps = []
    for core in range(nco):
        hf = core % 2
        sq = core // 2
        FtS, GtS, embS, decS, e0S, own, oth = per_half[hf]
        m = dict(wmap)
        m["xq"] = np.ascontiguousarray(np.stack([xp[2 * core], xp[2 * core + 1], xs[sq, own]]))
        m["xo"] = np.ascontiguousarray(xs[sq, oth])
        halo = np.zeros((4, 2, c.D), f32)
        if hf == 0:
            halo[2, 1] = xs[sq, Lq]
            halo[3, 0] = xs[sq, Lq - 1]
        else:
            halo[2, 0] = xs[sq, Lq - 1]
            halo[3, 1] = xs[sq, Lq]
        m["xhalo"] = halo
        m["pq"] = np.ascontiguousarray(np.stack([pp[2 * core], pp[2 * core + 1], ps[sq, own]]))
        m["rope"] = np.ascontiguousarray(np.stack([rope_table(ordP), rope_table(ordP), rope_table(own), rope_table(oth)]))
        m.update(FtP=FtP, GtP=GtP, FtS=FtS, GtS=GtS, embP=embP, embS=embS, decP=decP, decS=decS, e0P=e0P, e0S=e0S)
        in_maps.append(m)
    return in_maps


def assemble(cfg, results, n_prompt, n_sample):
    c = cfg
    Lq = c.Lq
    yp = np.empty((n_prompt, Lq, c.D), np.float32)
    ys = np.empty((n_sample, 2 * Lq, c.D), np.float32)
    for core, r in enumerate(results):
        y = np.asarray(r["yq"], np.float32)
        yp[2 * core] = y[0]
        yp[2 * core + 1] = y[1]
        hf = core % 2
        ys[core // 2, hf * Lq:(hf + 1) * Lq] = y[2]
    return yp, ys


_NC_CACHE = {}


def kernel(**inputs):
    cfg = FULL
    if "full" not in _NC_CACHE:
        _NC_CACHE["full"] = build(cfg)
    nc = _NC_CACHE["full"]
    in_maps = make_in_maps(cfg, inputs)
    res = run_bass_kernel_spmd(nc, in_maps, core_ids=list(range(cfg.ncores)))
    return assemble(cfg, res.results, 2 * cfg.ncores, cfg.ncores // 2)
```
